# Optimizing a Trainium2 kernel written in Bass

```python
import math
import jax
import jax.numpy as jnp
from jax import lax
import numpy as np

D_MODEL = 1024
BATCH = 4
SEQ = 8192
DEPTH = 4

GRID_W = 64
CTX_LEN = 256
N_MIXERS = 3
EPS = 1e-6
ROPE_BASE = 10000.0
Q_BLOCK = 128
N_MOD = 6

MLA_HEADS = 8
MLA_NOPE = 128
MLA_ROPE = 64
MLA_V = 128
MLA_Q_RANK = 384
MLA_KV_RANK = 256

HG_HEADS = 8
HG_DK = D_MODEL // HG_HEADS
HG_DV = D_MODEL // HG_HEADS
HG_CHUNK = 64

DF_HEADS = 8
DF_DQK = D_MODEL // (2 * DF_HEADS)
DF_DV = 2 * DF_DQK

D_FF = -(-(8 * D_MODEL) // (3 * 256)) * 256

N_MLA = (DEPTH + N_MIXERS - 1) // N_MIXERS
N_HG = (DEPTH + N_MIXERS - 2) // N_MIXERS
N_DF = DEPTH // N_MIXERS

kernel_name = 'hybrid_mla_hgrn2_diffattn_dit'


def rms_norm(x, w):
    xf = x.astype(jnp.float32)
    y = xf * lax.rsqrt(jnp.mean(xf * xf, axis=-1, keepdims=True) + EPS)
    return (y * w.astype(jnp.float32)).astype(x.dtype)


def modulate(x, shift, scale):
    return x * (1 + scale) + shift


def swiglu(a, w1, w3, w2):
    return (jax.nn.silu(a @ w1) * (a @ w3)) @ w2


def split_heads(t, n_heads):
    b, n, _ = t.shape
    return t.reshape(b, n, n_heads, -1).transpose(0, 2, 1, 3)


def merge_heads(t):
    b, h, n, d = t.shape
    return t.transpose(0, 2, 1, 3).reshape(b, n, h * d)


def axial_rope_tables(rows, rot_dim):
    row = jnp.repeat(jnp.arange(rows, dtype=jnp.float32), GRID_W)
    col = jnp.tile(jnp.arange(GRID_W, dtype=jnp.float32), rows)
    axis_dim = rot_dim // 2
    inv_freq = jnp.power(ROPE_BASE, -jnp.arange(0, axis_dim, 2, dtype=jnp.float32) / axis_dim)
    ang_r = row[:, None] * inv_freq
    ang_c = col[:, None] * inv_freq
    ang = jnp.concatenate([ang_r, ang_r, ang_c, ang_c], axis=-1)
    return jnp.cos(ang), jnp.sin(ang)


def apply_axial_rope(t, cos, sin):
    t1, t2, t3, t4 = jnp.split(t, 4, axis=-1)
    rot = jnp.concatenate([-t2, t1, -t4, t3], axis=-1)
    return (t * cos + rot * sin).astype(t.dtype)


def softmax_attend(q, k, v, scale):
    s = jnp.einsum('bhqd,bhkd->bhqk', q, k, preferred_element_type=jnp.float32) * scale
    p = jax.nn.softmax(s, axis=-1)
    return jnp.einsum('bhqk,bhkd->bhqd', p.astype(v.dtype), v)


def sweep_query_blocks(fn, q):
    n, d = q.shape[-2], q.shape[-1]
    nb = n // Q_BLOCK
    qb = jnp.moveaxis(q.reshape(q.shape[:-2] + (nb, Q_BLOCK, d)), -3, 0)
    o = jnp.moveaxis(lax.map(fn, qb), 0, -3)
    return o.reshape(o.shape[:-3] + (n, o.shape[-1]))


def gla_chunk_scan(q, k, v, log_f, s0):
    b, h, n, _ = q.shape
    nc = n // HG_CHUNK

    def chunks(t):
        return jnp.moveaxis(t.reshape(b, h, nc, HG_CHUNK, t.shape[-1]), 2, 0)

    order = jnp.tril(jnp.ones((HG_CHUNK, HG_CHUNK), dtype=bool))[:, :, None]

    def step(state, inp):
        qc, kc, vc, lfc = inp
        cum = jnp.cumsum(lfc, axis=-2)
        rel = jnp.where(order, cum[..., :, None, :] - cum[..., None, :, :], -jnp.inf)
        att = jnp.einsum('bhtd,bhsd,bhtsd->bhts', qc, kc, jnp.exp(rel))
        o = (jnp.einsum('bhts,bhse->bhte', att, vc)
             + jnp.einsum('bhtd,bhde->bhte', qc * jnp.exp(cum), state))
        last = cum[..., -1:, :]
        state = (jnp.exp(last[..., 0, :])[..., None] * state
                 + jnp.einsum('bhsd,bhse->bhde', kc * jnp.exp(last - cum), vc))
        return state, o

    state, o = lax.scan(step, s0, (chunks(q), chunks(k), chunks(v), chunks(log_f)))
    o = jnp.moveaxis(o, 0, 2).reshape(b, h, n, v.shape[-1])
    return o, state


def mla_mixer(a_lat, a_ctx, rope, w_in, q_norm_w, kv_norm_w, w_uq, w_ukv, qn_w, qr_w, kn_w, kr_w, w_o, update_ctx):
    cos, sin = rope

    def project(a):
        b, n, _ = a.shape
        cq, ckv, kr = jnp.split(a @ w_in, [MLA_Q_RANK, MLA_Q_RANK + MLA_KV_RANK], axis=-1)
        q = split_heads(rms_norm(cq, q_norm_w) @ w_uq, MLA_HEADS)
        kv = split_heads(rms_norm(ckv, kv_norm_w) @ w_ukv, MLA_HEADS)
        q_nope = rms_norm(q[..., :MLA_NOPE], qn_w)
        q_rope = rms_norm(q[..., MLA_NOPE:], qr_w)
        k_nope = rms_norm(kv[..., :MLA_NOPE], kn_w)
        v = kv[..., MLA_NOPE:]
        k_rope = rms_norm(kr, kr_w)[:, None]
        return q_nope, q_rope, k_nope, k_rope, v

    def full_k(kn, kr):
        return jnp.concatenate([kn, jnp.broadcast_to(kr, kn.shape[:-1] + (MLA_ROPE,))], axis=-1)

    qn_l, qr_l, kn_l, kr_l, v_l = project(a_lat)
    qr_l = apply_axial_rope(qr_l, cos, sin)
    kr_l = apply_axial_rope(kr_l, cos, sin)
    qn_c, qr_c, kn_c, kr_c, v_c = project(a_ctx)
    k_c = full_k(kn_c, kr_c)
    k_all = jnp.concatenate([k_c, full_k(kn_l, kr_l)], axis=-2)
    v_all = jnp.concatenate([v_c, v_l], axis=-2)
    scale = 1.0 / math.sqrt(MLA_NOPE + MLA_ROPE)
    q_l = jnp.concatenate([qn_l, qr_l], axis=-1)
    o_l = sweep_query_blocks(lambda qb: softmax_attend(qb, k_all, v_all, scale), q_l)
    out_l = merge_heads(o_l) @ w_o
    out_c = None
    if update_ctx:
        q_c = jnp.concatenate([qn_c, qr_c], axis=-1)
        out_c = merge_heads(softmax_attend(q_c, k_c, v_c, scale)) @ w_o
    return out_l, out_c


def hgrn2_mixer(a_lat, a_ctx, layer_idx, w_in, lb_logits, o_norm_w, w_o, update_ctx):
    lb_cum = jnp.cumsum(jax.nn.softmax(lb_logits.astype(jnp.float32), axis=0), axis=0)
    lb = lb_cum[layer_idx] - lb_cum[0]

    def project(a):
        q, zf, zb, i, g = jnp.split(a @ w_in, 5, axis=-1)

        def heads(t):
            return split_heads(t, HG_HEADS).astype(jnp.float32)

        def log_forget(z, lbd):
            lbd = lbd.reshape(HG_HEADS, 1, HG_DK)
            return jnp.logaddexp(jnp.log(lbd), jnp.log1p(-lbd) + jax.nn.log_sigmoid(heads(z)))

        return heads(q), heads(i), log_forget(zf, lb[0]), log_forget(zb, lb[1]), g

    def scan_dir(q, i, log_f, s0):
        return gla_chunk_scan(q, -jnp.expm1(log_f), i, log_f, s0)

    def rev(t):
        return jnp.flip(t, axis=-2)

    def readout(o, g):
        o = rms_norm(o.astype(a_lat.dtype), o_norm_w)
        return (merge_heads(o) * jax.nn.silu(g)) @ w_o

    q_l, i_l, lf_l, lbk_l, g_l = project(a_lat)
    q_c, i_c, lf_c, lbk_c, g_c = project(a_ctx)
    s0 = jnp.zeros((a_ctx.shape[0], HG_HEADS, HG_DK, HG_DV), jnp.float32)
    o_cf, s_cf = scan_dir(q_c, i_c, lf_c, s0)
    o_cb, s_cb = scan_dir(rev(q_c), rev(i_c), rev(lbk_c), s0)
    o_lf, _ = scan_dir(q_l, i_l, lf_l, s_cf)
    o_lb, _ = scan_dir(rev(q_l), rev(i_l), rev(lbk_l), s_cb)
    out_l = readout(o_lf + rev(o_lb), g_l)
    out_c = readout(o_cf + rev(o_cb), g_c) if update_ctx else None
    return out_l, out_c


def diff_mixer(a_lat, a_ctx, rope, layer_idx, w_qkv, qn_w, kn_w, lam_vec, sub_norm_w, w_o, update_ctx):
    cos, sin = rope
    lam_init = 0.8 - 0.6 * math.exp(-0.3 * layer_idx)
    lv = lam_vec.astype(jnp.float32)
    lam = jnp.exp(jnp.sum(lv[0] * lv[1])) - jnp.exp(jnp.sum(lv[2] * lv[3])) + lam_init

    def project(a):
        b, n, _ = a.shape
        q, k, v = jnp.split(a @ w_qkv, 3, axis=-1)
        q = rms_norm(q.reshape(b, n, DF_HEADS, 2, DF_DQK).transpose(0, 2, 3, 1, 4), qn_w)
        k = rms_norm(k.reshape(b, n, DF_HEADS, 2, DF_DQK).transpose(0, 2, 3, 1, 4), kn_w)
        return q, k, split_heads(v, DF_HEADS)

    scale = 1.0 / math.sqrt(DF_DQK)

    def diff_attend(q, k, v):
        s = jnp.einsum('bhmqd,bhmkd->bhmqk', q, k, preferred_element_type=jnp.float32) * scale
        p = jax.nn.softmax(s, axis=-1)
        a = p[:, :, 0] - lam * p[:, :, 1]
        return jnp.einsum('bhqk,bhkd->bhqd', a.astype(v.dtype), v)

    def readout(o):
        return merge_heads(rms_norm(o, sub_norm_w) * (1.0 - lam_init)) @ w_o

    q_l, k_l, v_l = project(a_lat)
    q_l = apply_axial_rope(q_l, cos, sin)
    k_l = apply_axial_rope(k_l, cos, sin)
    q_c, k_c, v_c = project(a_ctx)
    k_all = jnp.concatenate([k_c, k_l], axis=-2)
    v_all = jnp.concatenate([v_c, v_l], axis=-2)
    out_l = readout(sweep_query_blocks(lambda qb: diff_attend(qb, k_all, v_all), q_l))
    out_c = readout(diff_attend(q_c, k_c, v_c)) if update_ctx else None
    return out_l, out_c


def setup_inputs(seed: int = 0) -> dict:
    key = jax.random.key(seed)
    ks = iter(jax.random.split(key, 40))
    D = D_MODEL

    def nrm(shape, scale):
        return scale * jax.random.normal(next(ks), shape, jnp.float32)

    def gain(shape):
        return 1.0 + nrm(shape, 0.02)

    return {
        'x': nrm((BATCH, SEQ, D), 1.0),
        'c': nrm((BATCH, D), 1.0),
        'ctx': nrm((BATCH, CTX_LEN, D), 1.0),
        'c_ctx': nrm((D,), 1.0),
        'ada_w': nrm((DEPTH, D, N_MOD * D), 0.5 * D ** -0.5),
        'ada_b': nrm((DEPTH, N_MOD * D), 0.02),
        'norm1_w': gain((DEPTH, D)),
        'norm2_w': gain((DEPTH, D)),
        'ffn_w1': nrm((DEPTH, D, D_FF), D ** -0.5),
        'ffn_w3': nrm((DEPTH, D, D_FF), D ** -0.5),
        'ffn_w2': nrm((DEPTH, D_FF, D), D_FF ** -0.5),
        'mla_w_in': nrm((N_MLA, D, MLA_Q_RANK + MLA_KV_RANK + MLA_ROPE), D ** -0.5),
        'mla_q_norm_w': gain((N_MLA, MLA_Q_RANK)),
        'mla_kv_norm_w': gain((N_MLA, MLA_KV_RANK)),
        'mla_w_uq': nrm((N_MLA, MLA_Q_RANK, MLA_HEADS * (MLA_NOPE + MLA_ROPE)), MLA_Q_RANK ** -0.5),
        'mla_w_ukv': nrm((N_MLA, MLA_KV_RANK, MLA_HEADS * (MLA_NOPE + MLA_V)), MLA_KV_RANK ** -0.5),
        'mla_qn_w': gain((N_MLA, MLA_NOPE)),
        'mla_qr_w': gain((N_MLA, MLA_ROPE)),
        'mla_kn_w': gain((N_MLA, MLA_NOPE)),
        'mla_kr_w': gain((N_MLA, MLA_ROPE)),
        'mla_w_o': nrm((N_MLA, MLA_HEADS * MLA_V, D), (MLA_HEADS * MLA_V) ** -0.5),
        'hg_w_in': nrm((N_HG, D, 5 * D), D ** -0.5),
        'hg_lb_logits': nrm((DEPTH, 2, D), 1.0),
        'hg_o_norm_w': gain((N_HG, HG_DV)),
        'hg_w_o': nrm((N_HG, D, D), D ** -0.5),
        'df_w_qkv': nrm((N_DF, D, 3 * D), D ** -0.5),
        'df_qn_w': gain((N_DF, DF_DQK)),
        'df_kn_w': gain((N_DF, DF_DQK)),
        'df_lambda': nrm((N_DF, 4, DF_DQK), 0.1),
        'df_sub_norm_w': gain((N_DF, DF_DV)),
        'df_w_o': nrm((N_DF, D, D), D ** -0.5),
    }


def reference(x, c, ctx, c_ctx, ada_w, ada_b, norm1_w, norm2_w, ffn_w1, ffn_w3, ffn_w2,
              mla_w_in, mla_q_norm_w, mla_kv_norm_w, mla_w_uq, mla_w_ukv, mla_qn_w, mla_qr_w, mla_kn_w,
              mla_kr_w, mla_w_o, hg_w_in, hg_lb_logits, hg_o_norm_w, hg_w_o,
              df_w_qkv, df_qn_w, df_kn_w, df_lambda, df_sub_norm_w, df_w_o):
    rows = x.shape[1] // GRID_W
    rope_mla = axial_rope_tables(rows, MLA_ROPE)
    rope_df = axial_rope_tables(rows, DF_DQK)
    h_ctx = ctx
    for i in range(DEPTH):
        kind, j = i % N_MIXERS, i // N_MIXERS
        update_ctx = i < DEPTH - 1
        mod_l = jnp.split((jax.nn.silu(c) @ ada_w[i] + ada_b[i])[:, None, :], N_MOD, axis=-1)
        mod_c = jnp.split((jax.nn.silu(c_ctx) @ ada_w[i] + ada_b[i])[None, None, :], N_MOD, axis=-1)
        a_l = modulate(rms_norm(x, norm1_w[i]), mod_l[0], mod_l[1])
        a_c = modulate(rms_norm(h_ctx, norm1_w[i]), mod_c[0], mod_c[1])
        if kind == 0:
            o_l, o_c = mla_mixer(a_l, a_c, rope_mla, mla_w_in[j], mla_q_norm_w[j], mla_kv_norm_w[j],
                                 mla_w_uq[j], mla_w_ukv[j], mla_qn_w[j], mla_qr_w[j], mla_kn_w[j],
                                 mla_kr_w[j], mla_w_o[j], update_ctx)
        elif kind == 1:
            o_l, o_c = hgrn2_mixer(a_l, a_c, i, hg_w_in[j], hg_lb_logits, hg_o_norm_w[j], hg_w_o[j], update_ctx)
        else:
            o_l, o_c = diff_mixer(a_l, a_c, rope_df, i, df_w_qkv[j], df_qn_w[j], df_kn_w[j], df_lambda[j],
                                  df_sub_norm_w[j], df_w_o[j], update_ctx)
        x = x + mod_l[2] * o_l
        x = x + mod_l[5] * swiglu(modulate(rms_norm(x, norm2_w[i]), mod_l[3], mod_l[4]),
                                  ffn_w1[i], ffn_w3[i], ffn_w2[i])
        if update_ctx:
            h_ctx = h_ctx + mod_c[2] * o_c
            h_ctx = h_ctx + mod_c[5] * swiglu(modulate(rms_norm(h_ctx, norm2_w[i]), mod_c[3], mod_c[4]),
                                              ffn_w1[i], ffn_w3[i], ffn_w2[i])
    return x
```

```python
import math
from contextlib import ExitStack
import numpy as np
import concourse.bass as bass
import concourse.mybir as mybir
from concourse.bass_utils import run_bass_kernel_spmd

F32 = mybir.dt.float32
BF16 = mybir.dt.bfloat16
AF = mybir.ActivationFunctionType
ALU = mybir.AluOpType

D = 1024
KC = 8
CTX = 256
DFF = 2816
FC = 22
EPS = 1e-6
ENG = ("pe", "act", "dve", "pool", "sp")
NDSEM = 24


class Buf:
    def __init__(self, t, name=""):
        self.t = t
        self.name = name
        self.lw = None
        self.rd = []

    def __getitem__(self, idx):
        return self.t[idx]


class Op:
    __slots__ = ("eng", "fn", "deps", "signal", "sigidx", "isdma", "dsem", "dval", "pre")

    def __init__(self, eng, fn, isdma):
        self.eng = eng
        self.fn = fn
        self.deps = []
        self.signal = False
        self.sigidx = 0
        self.isdma = isdma
        self.dsem = None
        self.dval = 0
        self.pre = None


class Prog:
    def __init__(self, nc, es):
        self.nc = nc
        self.cnt = {e: es.enter_context(nc.semaphore("cnt_" + e)) for e in ENG}
        self.cntval = {e: 0 for e in ENG}
        self.dsems = {q: [es.enter_context(nc.semaphore("d_%s_%d" % (q, i))) for i in range(NDSEM)]
                      for q in ("sp", "pool", "act")}
        self.dgen = {q: [0] * NDSEM for q in ("sp", "pool", "act")}
        self.drot = {q: 0 for q in ("sp", "pool", "act")}
        self.known = {e: {} for e in ENG}
        self.ops = []
        self.nops = 0

    def _track(self, op, r, w):
        eng = op.eng
        raw = set()
        deps = set()
        for b in r:
            if b.lw is not None:
                deps.add(b.lw)
                raw.add(b.lw)
        for b in w:
            if b.lw is not None:
                deps.add(b.lw)
            for x in b.rd:
                deps.add(x)
        for b in r:
            b.rd.append(op)
        for b in w:
            b.lw = op
            b.rd = []
        for d in deps:
            if d is op:
                continue
            if (not d.isdma) and (not op.isdma) and d.eng == eng:
                if eng == "pe":
                    continue
            op.deps.append(d)
            if not d.isdma:
                d.signal = True

    def op(self, eng, fn, r=(), w=()):
        o = Op(eng, fn, False)
        self._track(o, r, w)
        self.ops.append(o)
        return o

    def dma(self, q, fn, r=(), w=()):
        o = Op(q, fn, True)
        self._track(o, r, w)
        j = self.drot[q]
        self.drot[q] = (j + 1) % NDSEM
        g = self.dgen[q][j]
        o.pre = (j, 16 * g) if g > 0 else None
        self.dgen[q][j] = g + 1
        o.dsem = j
        o.dval = 16 * (g + 1)
        self.ops.append(o)
        return o

    def flush(self, final_wait=True):
        nc = self.nc
        ops = self.ops
        self.ops = []
        for o in ops:
            if (not o.isdma) and o.signal:
                self.cntval[o.eng] += 1
                o.sigidx = self.cntval[o.eng]
        per = {e: [o for o in ops if o.eng == e] for e in ENG}
        self.nops += len(ops)

        def emit(e, eng):
            known = self.known[e]
            for o in per[e]:
                waits = []
                if o.isdma and o.pre is not None:
                    waits.append((("d", e, o.pre[0]), self.dsems[e][o.pre[0]], o.pre[1]))
                for d in o.deps:
                    if d.isdma:
                        waits.append((("d", d.eng, d.dsem), self.dsems[d.eng][d.dsem], d.dval))
                    else:
                        waits.append((("c", d.eng), self.cnt[d.eng], d.sigidx))
                best = {}
                for k, s, v in waits:
                    if v > known.get(k, 0) and v > best.get(k, (None, 0))[1]:
                        best[k] = (s, v)
                for k, (s, v) in best.items():
                    eng.wait_ge(s, v)
                    known[k] = v
                ins = o.fn(eng)
                if o.isdma:
                    ins.then_inc(self.dsems[e][o.dsem], 16)
                elif o.signal:
                    ins.then_inc(self.cnt[e], 1)
            if e in self.dsems:
                for j in range(NDSEM):
                    v = 16 * self.dgen[e][j]
                    if v > known.get(("d", e, j), 0):
                        eng.wait_ge(self.dsems[e][j], v)
                        known[("d", e, j)] = v

        with nc.Block() as block:
            @block.tensor
            def _(eng):
                emit("pe", eng)

            @block.scalar
            def _(eng):
                emit("act", eng)

            @block.vector
            def _(eng):
                emit("dve", eng)

            @block.gpsimd
            def _(eng):
                emit("pool", eng)

            @block.sync
            def _(eng):
                emit("sp", eng)


_uid = [0]


def sb(es, nc, name, shape, dt):
    _uid[0] += 1
    name = "%s_u%d" % (name, _uid[0])
    return Buf(es.enter_context(nc.sbuf_tensor(name, list(shape), dt)), name)


def vec_layout():
    L = {}
    n = [0]

    def add(name, cnt):
        L[name] = n[0]
        n[0] += cnt
    add("cc", 16)
    for i in range(4):
        add("adab%d" % i, 48)
    for i in range(4):
        add("n1w%d" % i, 8)
    for i in range(4):
        add("n2w%d" % i, 8)
    for j in range(2):
        add("mqn%d" % j, 3)
        add("mkvn%d" % j, 2)
        add("mqnw%d" % j, 1)
        add("mqrw%d" % j, 1)
        add("mknw%d" % j, 1)
        add("mkrw%d" % j, 1)
    add("lb", 64)
    add("hgon", 1)
    add("dqn", 1)
    add("dkn", 1)
    add("dlam", 4)
    add("dsub", 1)
    return L, n[0]


VL, NVEC = vec_layout()
NVPAD = 384


def token_tiles(TLAT, width=512):
    tiles = [(0, CTX, True)] if width >= CTX else [(c, width, True) for c in range(0, CTX, width)]
    for c in range(CTX, CTX + TLAT, width):
        tiles.append((c, width, False))
    return tiles


def build(cfg):
    TLAT = cfg["TLAT"]
    NT = CTX + TLAT
    NCH = NT // 128
    depth = cfg["depth"]
    nc = bass.Bass("TRN2", target_bir_lowering=False)

    def din(name, shape, dt=F32):
        return nc.dram_tensor(name, list(shape), dt, kind="ExternalInput").ap()

    def dscr(name, shape, dt):
        return nc.dram_tensor(name, list(shape), dt, kind="Internal").ap()

    x_in = din("x", [TLAT, D])
    ctx_in = din("ctx", [CTX, D])
    vecs_in = din("vecs", [NVPAD, 128])
    ident_in = din("ident_in", [128, 128])
    rotm_in = din("rotm", [128, 128])
    rope_in = din("rope", [2, 128, NT])
    hgmask_in = din("hgmask", [2, 32, 256])
    ada_w = din("ada_w", [4, D, 6 * D])
    ffn_w1 = din("ffn_w1", [4, D, DFF])
    ffn_w3 = din("ffn_w3", [4, D, DFF])
    ffn_w2 = din("ffn_w2", [4, DFF, D])
    mla_w_in = din("mla_w_in", [2, D, 704])
    mla_w_uq = din("mla_w_uq", [2, 384, 1536])
    mla_w_ukv = din("mla_w_ukv", [2, 256, 2048])
    mla_w_o = din("mla_w_o", [2, D, D])
    hg_w_in = din("hg_w_in", [1, D, 5 * D])
    hg_w_o = din("hg_w_o", [1, D, D])
    df_w_qkv = din("df_w_qkv", [1, D, 3 * D])
    df_w_o = din("df_w_o", [1, D, D])
    out = nc.dram_tensor("out", [TLAT, D], F32, kind="ExternalOutput").ap()

    xT = dscr("xT", [D, NT], F32)
    QA = dscr("QA", [8, 128, NT], BF16)
    QB = dscr("QB", [4, 128, NT], BF16)
    KA = dscr("KA", [8, 128, NT], BF16)
    KB = dscr("KB", [64, NT], BF16)
    VV = dscr("VV", [NT, D], BF16)
    OT = dscr("OT", [D, NT], BF16)
    KF = dscr("KF", [2, 8, 128, NT], BF16)
    LF = dscr("LF", [2, 8, 128, NT], F32)
    GS = dscr("GS", [D, NT], BF16)
    OF = dscr("OF", [D, NT], F32)

    with ExitStack() as top:
        p = Prog(nc, top)
        PS = [Buf(top.enter_context(nc.psum_tensor("psb%d" % i, [128, 512], F32)), "ps%d" % i) for i in range(8)]
        psrot = [0]

        def nb():
            psrot[0] = (psrot[0] + 1) % 8
            return PS[psrot[0]]

        def MM(o, l, rh, st, sp_, r, w):
            p.op("pe", lambda e: e.matmul(o, l, rh, start=st, stop=sp_), r=r, w=w)

        def TR(o, i, idn, r, w):
            p.op("pe", lambda e: e.transpose(out=o, in_=i, identity=idn), r=r, w=w)

        def ACT(o, i, func, r, w, scale=None, bias=None):
            kw = {}
            if scale is not None:
                kw["scale"] = scale
            if bias is not None:
                kw["bias"] = bias
            p.op("act", lambda e: e.activation(out=o, in_=i, func=func, **kw), r=r, w=w)

        def TT(eng, o, a, b, op, r, w):
            p.op(eng, lambda e: e.tensor_tensor(out=o, in0=a, in1=b, op=op), r=r, w=w)

        def TS(eng, o, a, s1, s2, op0, op1, r, w):
            if op1 is None:
                p.op(eng, lambda e: e.tensor_scalar(out=o, in0=a, scalar1=s1, scalar2=None, op0=op0), r=r, w=w)
            else:
                p.op(eng, lambda e: e.tensor_scalar(out=o, in0=a, scalar1=s1, scalar2=s2, op0=op0, op1=op1), r=r, w=w)

        def STT(o, a, s, b, op0, op1, r, w):
            p.op("dve", lambda e: e.scalar_tensor_tensor(out=o, in0=a, scalar=s, in1=b, op0=op0, op1=op1), r=r, w=w)

        def CP(eng, o, i, r, w):
            if eng == "act":
                p.op("act", lambda e: e.activation(out=o, in_=i, func=AF.Copy), r=r, w=w)
            else:
                p.op(eng, lambda e: e.tensor_copy(out=o, in_=i), r=r, w=w)

        def RECIP(o, i, r, w):
            p.op("dve", lambda e: e.reciprocal(out=o, in_=i), r=r, w=w)

        def MSET(eng, o, v, w):
            p.op(eng, lambda e: e.memset(o, v), w=w)

        def LD(o, i, w, q="sp"):
            p.dma(q, lambda e: e.dma_start(out=o, in_=i), w=w)

        def ST(o, i, r, q="sp"):
            p.dma(q, lambda e: e.dma_start(out=o, in_=i), r=r)

        def LDW(o, i, w):
            p.dma("pool", lambda e: e.dma_start(out=o, in_=i, max_dma_last_dim=8192), w=w)

        ident = sb(top, nc, "ident_sb", [128, 128], F32)
        rotm = sb(top, nc, "rotm_sb", [128, 128], F32)
        ones_b = sb(top, nc, "ones_b", [128, 128], BF16)
        ones_blk = sb(top, nc, "ones_blk", [128, 128], BF16)
        ones_f = sb(top, nc, "ones_f", [128, 128], F32)
        identb = sb(top, nc, "identb", [128, 128], BF16)
        vecT = sb(top, nc, "vecT", [128, NVPAD], F32)
        sc = sb(top, nc, "silu_c", [128, 2, 8], F32)
        modT = [sb(top, nc, "modT%d" % i, [128, 2, 48], F32) for i in range(4)]
        g1 = [sb(top, nc, "g1_%d" % i, [128, 2, 8], F32) for i in range(4)]
        g2 = [sb(top, nc, "g2_%d" % i, [128, 2, 8], F32) for i in range(4)]
        LD(ident[:, :], ident_in[:, :], [ident])
        LD(rotm[:, :], rotm_in[:, :], [rotm])
        CP("dve", identb[:, :], ident[:, :], [ident], [identb])
        MSET("dve", ones_b[:, :], 1.0, [ones_b])
        MSET("dve", ones_f[:, :], 1.0, [ones_f])
        MSET("dve", ones_blk[:, :], 0.0, [ones_blk])
        MSET("dve", ones_blk[0:64, 0:64], 1.0, [ones_blk])
        MSET("dve", ones_blk[64:128, 64:128], 1.0, [ones_blk])
        with ExitStack() as es:
            vin = [sb(es, nc, "vin%d" % g, [128, 128], F32) for g in range(3)]
            for g in range(3):
                LD(vin[g][:, :], vecs_in[g * 128:(g + 1) * 128, :], [vin[g]])
                pb = nb()
                TR(pb[:, 0:128], vin[g][:, :], ident[:, :], [vin[g], ident], [pb])
                CP("dve", vecT[:, g * 128:(g + 1) * 128], pb[:, 0:128], [pb], [vecT])
            ACT(sc[:, :, :], vecT[:, VL["cc"]:VL["cc"] + 16].rearrange("p (s k) -> p s k", s=2), AF.Silu, [vecT], [sc])
            aw = [sb(es, nc, "aw%d" % i, [128, KC, 512], F32) for i in range(2)]
            n = 0
            for i in range(depth):
                pm = PS[i % 2]
                for j in range(12):
                    a_ = aw[n % 2]
                    n += 1
                    LD(a_[:, :, :], ada_w[i, :, j * 512:(j + 1) * 512].rearrange("(k p) n -> p k n", p=128), [a_])
                    for c4 in range(4):
                        ch = j * 4 + c4
                        for kc in range(KC):
                            MM(pm[:, 2 * ch:2 * ch + 2], a_[:, kc, c4 * 128:(c4 + 1) * 128], sc[:, :, kc],
                               kc == 0, kc == KC - 1, [a_, sc], [pm])
                for s in range(2):
                    TT("dve", modT[i][:, s, :], pm[:, 0:96].rearrange("p (j s) -> p j s", s=2)[:, :, s],
                       vecT[:, VL["adab%d" % i]:VL["adab%d" % i] + 48], ALU.add, [pm, vecT], [modT[i]])
                    STT(g1[i][:, s, :], modT[i][:, s, 8:16], 1.0, vecT[:, VL["n1w%d" % i]:VL["n1w%d" % i] + 8],
                        ALU.add, ALU.mult, [modT[i], vecT], [g1[i]])
                    STT(g2[i][:, s, :], modT[i][:, s, 32:40], 1.0, vecT[:, VL["n2w%d" % i]:VL["n2w%d" % i] + 8],
                        ALU.add, ALU.mult, [modT[i], vecT], [g2[i]])
            p.flush()

        def vcol(name, k=0):
            c = VL[name] + k
            return vecT[:, c:c + 1]

        with ExitStack() as es:
            xin = [sb(es, nc, "p0_x%d" % i, [128, D], F32) for i in range(2)]
            xo = [sb(es, nc, "p0_o%d" % i, [128, KC, 128], F32) for i in range(2)]
            n = 0
            for (src, nrows, base) in ((ctx_in, CTX, 0), (x_in, TLAT, CTX)):
                for t in range(nrows // 128):
                    xi = xin[n % 2]
                    xoo = xo[n % 2]
                    LD(xi[:, :], src[t * 128:(t + 1) * 128, :], [xi])
                    for half in range(2):
                        pt = nb()
                        for k in range(4):
                            kc = half * 4 + k
                            TR(pt[:, k * 128:(k + 1) * 128], xi[:, kc * 128:(kc + 1) * 128], ident[:, :], [xi, ident], [pt])
                        CP("dve" if half == 0 else "act", xoo[:, half * 4:(half + 1) * 4, :],
                           pt[:, :].rearrange("p (k t) -> p k t", k=4), [pt], [xoo])
                    col = base + t * 128
                    ST(xT[:, col:col + 128].rearrange("(k p) t -> p k t", p=128), xoo[:, :, :], [xoo])
                    n += 1
            p.flush()

        def norm_mod(xs, w, gb, s, shbase, mt, a, sqb, sqap, rstd, tmp2):
            ACT(sqap(slice(0, KC)), xs[:, :, :w], AF.Square, [xs], [sqb])
            pb = nb()
            for kc in range(KC):
                MM(pb[:, :w], ones_b[:, :], sqap(kc), kc == 0, kc == KC - 1, [sqb, ones_b], [pb])
            ACT(rstd[:, :w], pb[:, :w], AF.Sqrt, [pb], [rstd], scale=1.0 / D, bias=EPS)
            RECIP(rstd[:, :w], rstd[:, :w], [rstd], [rstd])
            for kc in range(KC):
                t_ = tmp2[kc % len(tmp2)]
                TT("dve", t_[:, :w], xs[:, kc, :w], rstd[:, :w], ALU.mult, [xs, rstd], [t_])
                ACT(a[:, kc, :w], t_[:, :w], AF.Identity, [t_, gb, mt], [a],
                    scale=gb[:, s, kc:kc + 1], bias=mt[:, s, shbase + kc:shbase + kc + 1])

        def sub_rstd(src_ap, srcbufs, P_, w, inv_n, ones_ap, sq, rstd):
            ACT(sq[:P_, :w], src_ap, AF.Square, srcbufs, [sq])
            pb = nb()
            MM(pb[:P_, :w], ones_ap, sq[:P_, :w], True, True, [sq, ones_b, ones_blk], [pb])
            ACT(rstd[:P_, :w], pb[:P_, :w], AF.Sqrt, [pb], [rstd], scale=inv_n, bias=EPS)
            RECIP(rstd[:P_, :w], rstd[:P_, :w], [rstd], [rstd])

        def phase_po1(i, w_o_dram, last):
            with ExitStack() as es:
                wo = [sb(es, nc, "wo%d" % kc, [128, D], BF16) for kc in range(KC)]
                for kc in range(KC):
                    LDW(wo[kc][:, :], w_o_dram[kc * 128:(kc + 1) * 128, :], [wo[kc]])
                xs2 = [sb(es, nc, "po1_x%d" % k, [128, KC, 512], F32) for k in range(2)]
                ot2 = [sb(es, nc, "po1_o%d" % k, [128, KC, 512], BF16) for k in range(2)]
                tiles = [t for t in token_tiles(TLAT) if not (last and t[2])]

                def load(n):
                    c0, w, isc = tiles[n]
                    LD(xs2[n % 2][:, :, :w], xT[:, c0:c0 + w].rearrange("(k p) t -> p k t", p=128), [xs2[n % 2]])
                    LD(ot2[n % 2][:, :, :w], OT[:, c0:c0 + w].rearrange("(k p) t -> p k t", p=128), [ot2[n % 2]])
                load(0)
                for n in range(len(tiles)):
                    if n + 1 < len(tiles):
                        load(n + 1)
                    c0, w, isc = tiles[n]
                    s = 1 if isc else 0
                    xs, ot = xs2[n % 2], ot2[n % 2]
                    for mc in range(KC):
                        pb = nb()
                        for kc in range(KC):
                            MM(pb[:, :w], wo[kc][:, mc * 128:(mc + 1) * 128], ot[:, kc, :w], kc == 0, kc == KC - 1,
                               [wo[kc], ot], [pb])
                        STT(xs[:, mc, :w], pb[:, :w], modT[i][:, s, 16 + mc:17 + mc], xs[:, mc, :w], ALU.mult, ALU.add,
                            [pb, xs, modT[i]], [xs])
                    ST(xT[:, c0:c0 + w].rearrange("(k p) t -> p k t", p=128), xs[:, :, :w], [xs])
                p.flush()

        def phase_po2(i, last):
            W = 256
            with ExitStack() as es:
                w1 = [sb(es, nc, "w1_%d" % kc, [128, DFF], BF16) for kc in range(KC)]
                w3 = [sb(es, nc, "w3_%d" % kc, [128, DFF], BF16) for kc in range(KC)]
                w2 = [sb(es, nc, "w2_%d" % fc, [128, D], BF16) for fc in range(FC)]
                for kc in range(KC):
                    LDW(w1[kc][:, :], ffn_w1[i, kc * 128:(kc + 1) * 128, :], [w1[kc]])
                    LDW(w3[kc][:, :], ffn_w3[i, kc * 128:(kc + 1) * 128, :], [w3[kc]])
                for fc in range(FC):
                    LDW(w2[fc][:, :], ffn_w2[i, fc * 128:(fc + 1) * 128, :], [w2[fc]])
                xs2 = [sb(es, nc, "po2_x%d" % k, [128, KC, W], F32) for k in range(2)]
                a2 = sb(es, nc, "po2_a", [128, KC, W], BF16)
                h = sb(es, nc, "po2_h", [128, FC, W], BF16)
                rstd = sb(es, nc, "po2_r", [128, W], F32)
                tmp2 = [sb(es, nc, "po2_t%d" % k, [128, W], F32) for k in range(2)]
                s1b = [sb(es, nc, "po2_s%d" % k, [128, W], BF16) for k in range(2)]
                tiles = [t for t in token_tiles(TLAT, W) if not (last and t[2])]

                def load(n):
                    c0, w, isc = tiles[n]
                    LD(xs2[n % 2][:, :, :w], xT[:, c0:c0 + w].rearrange("(k p) t -> p k t", p=128), [xs2[n % 2]])
                load(0)
                for n in range(len(tiles)):
                    if n + 1 < len(tiles):
                        load(n + 1)
                    c0, w, isc = tiles[n]
                    s = 1 if isc else 0
                    xs = xs2[n % 2]
                    norm_mod(xs, w, g2[i], s, 24, modT[i], a2, h, lambda k: h[:, k, :w], rstd, tmp2)
                    for fc in range(FC):
                        p1, p3 = nb(), nb()
                        for kc in range(KC):
                            MM(p1[:, :w], w1[kc][:, fc * 128:(fc + 1) * 128], a2[:, kc, :w], kc == 0, kc == KC - 1,
                               [w1[kc], a2], [p1])
                        for kc in range(KC):
                            MM(p3[:, :w], w3[kc][:, fc * 128:(fc + 1) * 128], a2[:, kc, :w], kc == 0, kc == KC - 1,
                               [w3[kc], a2], [p3])
                        s1 = s1b[fc % 2]
                        ACT(s1[:, :w], p1[:, :w], AF.Silu, [p1], [s1])
                        TT("dve", h[:, fc, :w], s1[:, :w], p3[:, :w], ALU.mult, [s1, p3], [h])
                    for mc in range(KC):
                        pb = nb()
                        for fc in range(FC):
                            MM(pb[:, :w], w2[fc][:, mc * 128:(mc + 1) * 128], h[:, fc, :w], fc == 0, fc == FC - 1,
                               [w2[fc], h], [pb])
                        STT(xs[:, mc, :w], pb[:, :w], modT[i][:, s, 40 + mc:41 + mc], xs[:, mc, :w], ALU.mult, ALU.add,
                            [pb, xs, modT[i]], [xs])
                    ST(xT[:, c0:c0 + w].rearrange("(k p) t -> p k t", p=128), xs[:, :, :w], [xs])
                p.flush()

        def rope_apply(t_, w, cs, ob, r1, r2):
            pr = nb()
            MM(pr[:, :w], rotm[:, :], t_[:, :w], True, True, [rotm, t_], [pr])
            TT("dve", r1[:, :w], t_[:, :w], cs[:, 0, :w], ALU.mult, [t_, cs], [r1])
            TT("dve", r2[:, :w], pr[:, :w], cs[:, 1, :w], ALU.mult, [pr, cs], [r2])
            TT("pool", ob[:, :w], r1[:, :w], r2[:, :w], ALU.add, [r1, r2], [ob])

        def phase_mla_proj(i, j, last):
            with ExitStack() as es:
                win = [sb(es, nc, "mwin%d" % kc, [128, 704], BF16) for kc in range(KC)]
                wuq = [sb(es, nc, "mwuq%d" % k, [128, 1536], BF16) for k in range(3)]
                wukv = [sb(es, nc, "mwukv%d" % k, [128, 2048], BF16) for k in range(2)]
                for kc in range(KC):
                    LDW(win[kc][:, :], mla_w_in[j, kc * 128:(kc + 1) * 128, :], [win[kc]])
                for k in range(3):
                    LDW(wuq[k][:, :], mla_w_uq[j, k * 128:(k + 1) * 128, :], [wuq[k]])
                for k in range(2):
                    LDW(wukv[k][:, :], mla_w_ukv[j, k * 128:(k + 1) * 128, :], [wukv[k]])
                xs2 = [sb(es, nc, "mp_x%d" % k, [128, KC, 512], F32) for k in range(2)]
                cs2 = [sb(es, nc, "mp_cs%d" % k, [128, 2, 512], F32) for k in range(2)]
                a = sb(es, nc, "mp_a", [128, KC, 512], BF16)
                sq8 = sb(es, nc, "mp_sq8", [128, KC, 512], BF16)
                rstd = sb(es, nc, "mp_r", [128, 512], F32)
                tmp2 = [sb(es, nc, "mp_t%d" % k, [128, 512], F32) for k in range(4)]
                cqn = sb(es, nc, "mp_cqn", [128, 3, 512], BF16)
                ckvn = sb(es, nc, "mp_ckvn", [128, 2, 512], BF16)
                sq = [sb(es, nc, "mp_sq%d" % k, [128, 512], BF16) for k in range(4)]
                rs = [sb(es, nc, "mp_rs%d" % k, [128, 512], F32) for k in range(4)]
                ob = [sb(es, nc, "mp_ob%d" % k, [128, 512], BF16) for k in range(4)]
                tf = [sb(es, nc, "mp_tf%d" % k, [128, 512], F32) for k in range(4)]
                r1 = [sb(es, nc, "mp_r1%d" % k, [128, 512], F32) for k in range(2)]
                r2 = [sb(es, nc, "mp_r2%d" % k, [128, 512], F32) for k in range(2)]
                vt = [sb(es, nc, "mp_vt%d" % k, [128, D], BF16) for k in range(2)]
                tiles = token_tiles(TLAT)
                cnt = [0]

                def load(n):
                    c0, w, isc = tiles[n]
                    LD(xs2[n % 2][:, :, :w], xT[:, c0:c0 + w].rearrange("(k p) t -> p k t", p=128), [xs2[n % 2]])
                    LD(cs2[n % 2][:, :, :w], rope_in[:, :, c0:c0 + w].rearrange("c p t -> p c t"), [cs2[n % 2]])
                load(0)
                for n in range(len(tiles)):
                    if n + 1 < len(tiles):
                        load(n + 1)
                    c0, w, isc = tiles[n]
                    s = 1 if isc else 0
                    xs, cs = xs2[n % 2], cs2[n % 2]
                    need_q = not (last and isc)
                    norm_mod(xs, w, g1[i], s, 0, modT[i], a, sq8, lambda k: sq8[:, k, :w], rstd, tmp2)
                    pc = []
                    for c in range(6):
                        if c < 3 and not need_q:
                            pc.append(None)
                            continue
                        pb = nb()
                        M = 64 if c == 5 else 128
                        for kc in range(KC):
                            MM(pb[:M, :w], win[kc][:, c * 128:c * 128 + M], a[:, kc, :w], kc == 0, kc == KC - 1,
                               [win[kc], a], [pb])
                        pc.append(pb)
                    for (lo, hi, inv, wname, dst) in ((0, 3, 1.0 / 384, "mqn%d" % j, cqn), (3, 5, 1.0 / 256, "mkvn%d" % j, ckvn)):
                        if lo == 0 and not need_q:
                            continue
                        pr = nb()
                        for c in range(lo, hi):
                            q_ = sq[cnt[0] % 4]
                            cnt[0] += 1
                            ACT(q_[:, :w], pc[c][:, :w], AF.Square, [pc[c]], [q_])
                            MM(pr[:, :w], ones_b[:, :], q_[:, :w], c == lo, c == hi - 1, [q_, ones_b], [pr])
                        r_ = rs[cnt[0] % 4]
                        ACT(r_[:, :w], pr[:, :w], AF.Sqrt, [pr], [r_], scale=inv, bias=EPS)
                        RECIP(r_[:, :w], r_[:, :w], [r_], [r_])
                        for c in range(lo, hi):
                            STT(dst[:, c - lo, :w], pc[c][:, :w], vcol(wname, c - lo), r_[:, :w], ALU.mult, ALU.mult,
                                [pc[c], vecT, r_], [dst])
                    q_, r_ = sq[cnt[0] % 4], rs[cnt[0] % 4]
                    cnt[0] += 1
                    sub_rstd(pc[5][:64, :w], [pc[5]], 64, w, 1.0 / 64, ones_b[0:64, 0:64], q_, r_)
                    t_ = tf[cnt[0] % 4]
                    MSET("pool", t_[64:128, :w], 0.0, [t_])
                    STT(t_[:64, :w], pc[5][:64, :w], vecT[0:64, VL["mkrw%d" % j]:VL["mkrw%d" % j] + 1], r_[:64, :w], ALU.mult, ALU.mult,
                        [pc[5], vecT, r_], [t_])
                    o_ = ob[cnt[0] % 4]
                    rope_apply(t_, w, cs, o_, r1[cnt[0] % 2], r2[cnt[0] % 2])
                    ST(KB[:, c0:c0 + w], o_[0:64, :w], [o_])
                    if need_q:
                        for ch in range(12):
                            pb = nb()
                            for k in range(3):
                                MM(pb[:, :w], wuq[k][:, ch * 128:(ch + 1) * 128], cqn[:, k, :w], k == 0, k == 2,
                                   [wuq[k], cqn], [pb])
                            q_, r_ = sq[cnt[0] % 4], rs[cnt[0] % 4]
                            cnt[0] += 1
                            o_ = ob[cnt[0] % 4]
                            if ch < 8:
                                sub_rstd(pb[:, :w], [pb], 128, w, 1.0 / 128, ones_b[:, :], q_, r_)
                                STT(o_[:, :w], pb[:, :w], vcol("mqnw%d" % j), r_[:, :w], ALU.mult, ALU.mult,
                                    [pb, vecT, r_], [o_])
                                ST(QA[ch, :, c0:c0 + w], o_[:, :w], [o_])
                            else:
                                sub_rstd(pb[:, :w], [pb], 128, w, 1.0 / 64, ones_blk[:, :], q_, r_)
                                t_ = tf[cnt[0] % 4]
                                STT(t_[:, :w], pb[:, :w], vcol("mqrw%d" % j), r_[:, :w], ALU.mult, ALU.mult,
                                    [pb, vecT, r_], [t_])
                                rope_apply(t_, w, cs, o_, r1[cnt[0] % 2], r2[cnt[0] % 2])
                                ST(QB[ch - 8, :, c0:c0 + w], o_[:, :w], [o_])
                    for hh in range(8):
                        pb = nb()
                        for k in range(2):
                            MM(pb[:, :w], wukv[k][:, hh * 128:(hh + 1) * 128], ckvn[:, k, :w], k == 0, k == 1,
                               [wukv[k], ckvn], [pb])
                        q_, r_ = sq[cnt[0] % 4], rs[cnt[0] % 4]
                        cnt[0] += 1
                        o_ = ob[cnt[0] % 4]
                        sub_rstd(pb[:, :w], [pb], 128, w, 1.0 / 128, ones_b[:, :], q_, r_)
                        STT(o_[:, :w], pb[:, :w], vcol("mknw%d" % j), r_[:, :w], ALU.mult, ALU.mult, [pb, vecT, r_], [o_])
                        ST(KA[hh, :, c0:c0 + w], o_[:, :w], [o_])
                    for sub in range(w // 128):
                        v_ = vt[cnt[0] % 2]
                        cnt[0] += 1
                        for half in range(2):
                            pb = nb()
                            for k in range(2):
                                MM(pb[:, :], ckvn[:, k, sub * 128:(sub + 1) * 128],
                                   wukv[k][:, 1024 + half * 512:1024 + (half + 1) * 512], k == 0, k == 1,
                                   [wukv[k], ckvn], [pb])
                            CP("act" if half == 0 else "dve", v_[:, half * 512:(half + 1) * 512], pb[:, :], [pb], [v_])
                        ST(VV[c0 + sub * 128:c0 + (sub + 1) * 128, :], v_[:, :], [v_])
                p.flush()

        def phase_mla_attn(last):
            scale = 1.0 / math.sqrt(192.0)
            with ExitStack() as es:
                kr = sb(es, nc, "ma_kr", [64, NT], BF16)
                kn2 = [sb(es, nc, "ma_kn%d" % k, [128, NT], BF16) for k in range(2)]
                vh2 = [sb(es, nc, "ma_v%d" % k, [128, NCH, 128], BF16) for k in range(2)]
                qn2 = [sb(es, nc, "ma_qn%d" % k, [128, 512], BF16) for k in range(2)]
                qr2 = [sb(es, nc, "ma_qr%d" % k, [64, 512], BF16) for k in range(2)]
                pt3 = [sb(es, nc, "ma_p%d" % k, [128, 512], BF16) for k in range(6)]
                gb = [sb(es, nc, "ma_g%d" % k, [128, 512], BF16) for k in range(6)]
                gcnt = [0]

                def sum_group(grp, w, psm, first, lastg):
                    def nxt():
                        gcnt[0] += 1
                        return gb[gcnt[0] % 6]
                    cur = list(grp)
                    while len(cur) > 1:
                        nx = []
                        for a_ in range(0, len(cur) - 1, 2):
                            g_ = nxt()
                            TT("dve", g_[:, :w], cur[a_][:, :w], cur[a_ + 1][:, :w], ALU.add, [cur[a_], cur[a_ + 1]], [g_])
                            nx.append(g_)
                        if len(cur) % 2:
                            nx.append(cur[-1])
                        cur = nx
                    MM(psm[:, :w], ones_b[:, :], cur[0][:, :w], first, lastg, [ones_b, cur[0]], [psm])
                rsum = sb(es, nc, "ma_rs", [128, 512], F32)
                ob2 = [sb(es, nc, "ma_ob%d" % k, [128, 512], BF16) for k in range(2)]
                LD(kr[:, :], KB[:, :], [kr])
                tiles = [t for t in token_tiles(TLAT) if not (last and t[2])]

                def loadh(hh):
                    LD(kn2[hh % 2][:, :], KA[hh, :, :], [kn2[hh % 2]])
                    for c_ in range(0, NCH, 16):
                        ce = min(NCH, c_ + 16)
                        LD(vh2[hh % 2][:, c_:ce, :], VV[c_ * 128:ce * 128, hh * 128:(hh + 1) * 128].rearrange("(c p) d -> p c d", p=128), [vh2[hh % 2]])
                work = [(hh, n) for hh in range(8) for n in range(len(tiles))]

                def loadq(k):
                    hh, n = work[k]
                    c0, w, isc = tiles[n]
                    LD(qn2[k % 2][:, :w], QA[hh, :, c0:c0 + w], [qn2[k % 2]])
                    LD(qr2[k % 2][:, :w], QB[hh // 2, (hh % 2) * 64:(hh % 2) * 64 + 64, c0:c0 + w], [qr2[k % 2]])
                loadh(0)
                loadq(0)
                pcnt = 0
                LA = 3
                for k in range(len(work)):
                    hh, n = work[k]
                    if n == 0 and hh + 1 < 8:
                        loadh(hh + 1)
                    if k + 1 < len(work):
                        loadq(k + 1)
                    c0, w, isc = tiles[n]
                    kn, vh, qn, qr = kn2[hh % 2], vh2[hh % 2], qn2[k % 2], qr2[k % 2]
                    nk = CTX // 128 if isc else NCH
                    po, psm = PS[(k % 2) * 2], PS[(k % 2) * 2 + 1]

                    def s_mm(c):
                        pss = PS[4 + (pcnt + c) % 4]
                        MM(pss[:, :w], kn[:, c * 128:(c + 1) * 128], qn[:, :w], True, False, [kn, qn], [pss])
                        MM(pss[:, :w], kr[:, c * 128:(c + 1) * 128], qr[:, :w], False, True, [kr, qr], [pss])
                    for c in range(min(LA, nk)):
                        s_mm(c)
                    grp = []
                    for c in range(nk):
                        if c + LA < nk:
                            s_mm(c + LA)
                        pss = PS[4 + (pcnt + c) % 4]
                        pt = pt3[(pcnt + c) % 6]
                        ACT(pt[:, :w], pss[:, :w], AF.Exp, [pss], [pt], scale=scale)
                        MM(po[:, :w], vh[:, c, :], pt[:, :w], c == 0, c == nk - 1, [vh, pt], [po])
                        grp.append(pt)
                        if len(grp) == 4 or c == nk - 1:
                            sum_group(grp, w, psm, c < 4, c == nk - 1)
                            grp = []
                    pcnt += nk
                    RECIP(rsum[:, :w], psm[:, :w], [psm], [rsum])
                    o_ = ob2[k % 2]
                    TT("dve", o_[:, :w], po[:, :w], rsum[:, :w], ALU.mult, [po, rsum], [o_])
                    ST(OT[hh * 128:(hh + 1) * 128, c0:c0 + w], o_[:, :w], [o_])
                p.flush()

        def phase_df_proj(i, last):
            with ExitStack() as es:
                wq = [sb(es, nc, "dwq%d" % kc, [128, 3 * D], BF16) for kc in range(KC)]
                for kc in range(KC):
                    LDW(wq[kc][:, :], df_w_qkv[0, kc * 128:(kc + 1) * 128, :], [wq[kc]])
                xs2 = [sb(es, nc, "dp_x%d" % k, [128, KC, 512], F32) for k in range(2)]
                cs2 = [sb(es, nc, "dp_cs%d" % k, [128, 2, 512], F32) for k in range(2)]
                a = sb(es, nc, "dp_a", [128, KC, 512], BF16)
                sq8 = sb(es, nc, "dp_sq8", [128, KC, 512], BF16)
                rstd = sb(es, nc, "dp_r", [128, 512], F32)
                tmp2 = [sb(es, nc, "dp_t%d" % k, [128, 512], F32) for k in range(4)]
                sq = [sb(es, nc, "dp_sq%d" % k, [128, 512], BF16) for k in range(4)]
                rs = [sb(es, nc, "dp_rs%d" % k, [128, 512], F32) for k in range(4)]
                ob = [sb(es, nc, "dp_ob%d" % k, [128, 512], BF16) for k in range(4)]
                tf = [sb(es, nc, "dp_tf%d" % k, [128, 512], F32) for k in range(4)]
                r1 = [sb(es, nc, "dp_r1%d" % k, [128, 512], F32) for k in range(2)]
                r2 = [sb(es, nc, "dp_r2%d" % k, [128, 512], F32) for k in range(2)]
                vt = [sb(es, nc, "dp_vt%d" % k, [128, D], BF16) for k in range(2)]
                tiles = token_tiles(TLAT)
                cnt = [0]

                def load(n):
                    c0, w, isc = tiles[n]
                    LD(xs2[n % 2][:, :, :w], xT[:, c0:c0 + w].rearrange("(k p) t -> p k t", p=128), [xs2[n % 2]])
                    LD(cs2[n % 2][:, :, :w], rope_in[:, :, c0:c0 + w].rearrange("c p t -> p c t"), [cs2[n % 2]])
                load(0)
                for n in range(len(tiles)):
                    if n + 1 < len(tiles):
                        load(n + 1)
                    c0, w, isc = tiles[n]
                    s = 1 if isc else 0
                    xs, cs = xs2[n % 2], cs2[n % 2]
                    norm_mod(xs, w, g1[i], s, 0, modT[i], a, sq8, lambda k: sq8[:, k, :w], rstd, tmp2)
                    for ch in range(16):
                        if ch < 8 and last and isc:
                            continue
                        pb = nb()
                        for kc in range(KC):
                            MM(pb[:, :w], wq[kc][:, ch * 128:(ch + 1) * 128], a[:, kc, :w], kc == 0, kc == KC - 1,
                               [wq[kc], a], [pb])
                        q_, r_ = sq[cnt[0] % 4], rs[cnt[0] % 4]
                        cnt[0] += 1
                        o_ = ob[cnt[0] % 4]
                        t_ = tf[cnt[0] % 4]
                        sub_rstd(pb[:, :w], [pb], 128, w, 1.0 / 64, ones_blk[:, :], q_, r_)
                        STT(t_[:, :w], pb[:, :w], vcol("dqn" if ch < 8 else "dkn"), r_[:, :w], ALU.mult, ALU.mult,
                            [pb, vecT, r_], [t_])
                        rope_apply(t_, w, cs, o_, r1[cnt[0] % 2], r2[cnt[0] % 2])
                        dst = QA if ch < 8 else KA
                        ST(dst[ch % 8, :, c0:c0 + w], o_[:, :w], [o_])
                    for sub in range(w // 128):
                        v_ = vt[cnt[0] % 2]
                        cnt[0] += 1
                        for half in range(2):
                            pb = nb()
                            for kc in range(KC):
                                MM(pb[:, :], a[:, kc, sub * 128:(sub + 1) * 128],
                                   wq[kc][:, 2048 + half * 512:2048 + (half + 1) * 512], kc == 0, kc == KC - 1,
                                   [wq[kc], a], [pb])
                            CP("act" if half == 0 else "dve", v_[:, half * 512:(half + 1) * 512], pb[:, :], [pb], [v_])
                        ST(VV[c0 + sub * 128:c0 + (sub + 1) * 128, :], v_[:, :], [v_])
                p.flush()

        def phase_df_attn(i, last):
            scale = 1.0 / math.sqrt(64.0)
            lam_init = 0.8 - 0.6 * math.exp(-0.3 * i)
            with ExitStack() as es:
                kd2 = [sb(es, nc, "da_k%d" % k, [128, NT], BF16) for k in range(2)]
                vh2 = [sb(es, nc, "da_v%d" % k, [128, NCH, 128], BF16) for k in range(2)]
                qd2 = [sb(es, nc, "da_q%d" % k, [128, 512], BF16) for k in range(4)]
                for k_ in range(2):
                    MSET("pool", qd2[2 * k_][64:128, :], 0.0, [qd2[2 * k_]])
                    MSET("pool", qd2[2 * k_ + 1][0:64, :], 0.0, [qd2[2 * k_ + 1]])
                pt4 = [sb(es, nc, "da_p%d" % k, [128, 512], BF16) for k in range(4)]
                rsum = [sb(es, nc, "da_rs%d" % k, [128, 512], F32) for k in range(2)]
                o01 = [sb(es, nc, "da_o%d" % k, [128, 512], F32) for k in range(2)]
                od = sb(es, nc, "da_od", [128, 512], F32)
                sqd = sb(es, nc, "da_sq", [128, 512], BF16)
                rsd = sb(es, nc, "da_rsd", [128, 512], F32)
                ob2 = [sb(es, nc, "da_ob%d" % k, [128, 512], BF16) for k in range(2)]
                lam = sb(es, nc, "da_lam", [128, 4], F32)
                wsub = sb(es, nc, "da_wsub", [128, 1], F32)
                c_ = VL["dlam"]
                TT("dve", lam[:, 0:1], vecT[:, c_:c_ + 1], vecT[:, c_ + 1:c_ + 2], ALU.mult, [vecT], [lam])
                TT("dve", lam[:, 1:2], vecT[:, c_ + 2:c_ + 3], vecT[:, c_ + 3:c_ + 4], ALU.mult, [vecT, lam], [lam])
                pl = nb()
                MM(pl[:, 0:2], ones_f[:, :], lam[:, 0:2], True, True, [ones_f, lam], [pl])
                ACT(lam[:, 2:4], pl[:, 0:2], AF.Exp, [pl], [lam])
                TT("dve", lam[:, 0:1], lam[:, 3:4], lam[:, 2:3], ALU.subtract, [lam], [lam])
                TS("dve", lam[:, 0:1], lam[:, 0:1], -lam_init, None, ALU.add, None, [lam], [lam])
                TS("dve", wsub[:, :], vcol("dsub"), 1.0 - lam_init, None, ALU.mult, None, [vecT], [wsub])
                tiles = [t for t in token_tiles(TLAT) if not (last and t[2])]

                def loadh(hh):
                    LD(kd2[hh % 2][:, :], KA[hh, :, :], [kd2[hh % 2]])
                    for c_ in range(0, NCH, 16):
                        ce = min(NCH, c_ + 16)
                        LD(vh2[hh % 2][:, c_:ce, :],
                           VV[c_ * 128:ce * 128, hh * 128:(hh + 1) * 128].rearrange("(c p) d -> p c d", p=128), [vh2[hh % 2]])
                work = [(hh, n) for hh in range(8) for n in range(len(tiles))]

                def loadq(k):
                    hh, n = work[k]
                    c0, w, isc = tiles[n]
                    LD(qd2[2 * (k % 2)][0:64, :w], QA[hh, 0:64, c0:c0 + w], [qd2[2 * (k % 2)]])
                    LD(qd2[2 * (k % 2) + 1][64:128, :w], QA[hh, 64:128, c0:c0 + w], [qd2[2 * (k % 2) + 1]])
                loadh(0)
                loadq(0)
                pcnt = 0
                for k in range(len(work)):
                    hh, n = work[k]
                    if n == 0 and hh + 1 < 8:
                        loadh(hh + 1)
                    if k + 1 < len(work):
                        loadq(k + 1)
                    c0, w, isc = tiles[n]
                    kd, vh = kd2[hh % 2], vh2[hh % 2]
                    qdm = (qd2[2 * (k % 2)], qd2[2 * (k % 2) + 1])
                    nk = CTX // 128 if isc else NCH
                    LA = 3
                    nj = 2 * nk

                    def s_mm(j_):
                        c, m = j_ // 2, j_ % 2
                        pss = PS[4 + (pcnt + j_) % 4]
                        MM(pss[:, :w], kd[:, c * 128:(c + 1) * 128], qdm[m][:, :w], True, True, [kd, qdm[m]], [pss])
                    for j_ in range(min(LA, nj)):
                        s_mm(j_)
                    for j_ in range(nj):
                        if j_ + LA < nj:
                            s_mm(j_ + LA)
                        c, m = j_ // 2, j_ % 2
                        pss = PS[4 + (pcnt + j_) % 4]
                        pt = pt4[(pcnt + j_) % 4]
                        po, psm = PS[2 * m], PS[2 * m + 1]
                        ACT(pt[:, :w], pss[:, :w], AF.Exp, [pss], [pt], scale=scale)
                        MM(po[:, :w], vh[:, c, :], pt[:, :w], c == 0, c == nk - 1, [vh, pt], [po])
                        MM(psm[:, :w], ones_b[:, :], pt[:, :w], c == 0, c == nk - 1, [ones_b, pt], [psm])
                    pcnt += nj
                    for m in range(2):
                        RECIP(rsum[m][:, :w], PS[2 * m + 1][:, :w], [PS[2 * m + 1]], [rsum[m]])
                        TT("dve", o01[m][:, :w], PS[2 * m][:, :w], rsum[m][:, :w], ALU.mult, [PS[2 * m], rsum[m]], [o01[m]])
                    STT(od[:, :w], o01[1][:, :w], lam[:, 0:1], o01[0][:, :w], ALU.mult, ALU.add, [o01[0], o01[1], lam], [od])
                    sub_rstd(od[:, :w], [od], 128, w, 1.0 / 128, ones_b[:, :], sqd, rsd)
                    o_ = ob2[k % 2]
                    STT(o_[:, :w], od[:, :w], wsub[:, 0:1], rsd[:, :w], ALU.mult, ALU.mult, [od, wsub, rsd], [o_])
                    ST(OT[hh * 128:(hh + 1) * 128, c0:c0 + w], o_[:, :w], [o_])
                p.flush()

        def phase_hg_proj(i):
            with ExitStack() as es:
                wh = [sb(es, nc, "hwin%d" % kc, [128, 5 * D], BF16) for kc in range(KC)]
                for kc in range(KC):
                    LDW(wh[kc][:, :], hg_w_in[0, kc * 128:(kc + 1) * 128, :], [wh[kc]])
                ex = sb(es, nc, "hp_ex", [128, 64], F32)
                den = sb(es, nc, "hp_den", [128, 16], F32)
                lb = sb(es, nc, "hp_lb", [128, 16], F32)
                oml = sb(es, nc, "hp_oml", [128, 16], F32)
                c_ = VL["lb"]
                ACT(ex[:, :], vecT[:, c_:c_ + 64], AF.Exp, [vecT], [ex])
                TT("dve", den[:, :], ex[:, 0:16], ex[:, 16:32], ALU.add, [ex], [den])
                TT("dve", den[:, :], den[:, :], ex[:, 32:48], ALU.add, [ex, den], [den])
                TT("dve", den[:, :], den[:, :], ex[:, 48:64], ALU.add, [ex, den], [den])
                RECIP(den[:, :], den[:, :], [den], [den])
                CP("dve", lb[:, :], ex[:, 16:32], [ex], [lb])
                for l in range(2, i + 1):
                    TT("dve", lb[:, :], lb[:, :], ex[:, l * 16:(l + 1) * 16], ALU.add, [ex, lb], [lb])
                TT("dve", lb[:, :], lb[:, :], den[:, :], ALU.mult, [lb, den], [lb])
                TS("dve", oml[:, :], lb[:, :], -1.0, 1.0, ALU.mult, ALU.add, [lb], [oml])
                xs2 = [sb(es, nc, "hp_x%d" % k, [128, KC, 512], F32) for k in range(2)]
                a = sb(es, nc, "hp_a", [128, KC, 512], BF16)
                sq8 = sb(es, nc, "hp_sq8", [128, KC, 512], BF16)
                rstd = sb(es, nc, "hp_r", [128, 512], F32)
                tmp2 = [sb(es, nc, "hp_t%d" % k, [128, 512], F32) for k in range(4)]
                ob = [sb(es, nc, "hp_ob%d" % k, [128, 512], BF16) for k in range(4)]
                sg = [sb(es, nc, "hp_sg%d" % k, [128, 512], F32) for k in range(3)]
                fb = [sb(es, nc, "hp_f%d" % k, [128, 512], F32) for k in range(3)]
                lfb = [sb(es, nc, "hp_lf%d" % k, [128, 512], F32) for k in range(3)]
                vt = [sb(es, nc, "hp_vt%d" % k, [128, D], BF16) for k in range(2)]
                tiles = token_tiles(TLAT)
                cnt = [0]

                def load(n):
                    c0, w, isc = tiles[n]
                    LD(xs2[n % 2][:, :, :w], xT[:, c0:c0 + w].rearrange("(k p) t -> p k t", p=128), [xs2[n % 2]])
                load(0)
                for n in range(len(tiles)):
                    if n + 1 < len(tiles):
                        load(n + 1)
                    c0, w, isc = tiles[n]
                    s = 1 if isc else 0
                    xs = xs2[n % 2]
                    norm_mod(xs, w, g1[i], s, 0, modT[i], a, sq8, lambda k: sq8[:, k, :w], rstd, tmp2)
                    for grp in (0, 1, 2, 4):
                        for hh in range(8):
                            ch = grp * 8 + hh
                            pb = nb()
                            for kc in range(KC):
                                MM(pb[:, :w], wh[kc][:, ch * 128:(ch + 1) * 128], a[:, kc, :w], kc == 0, kc == KC - 1,
                                   [wh[kc], a], [pb])
                            cnt[0] += 1
                            o_ = ob[cnt[0] % 4]
                            if grp == 0:
                                CP("dve", o_[:, :w], pb[:, :w], [pb], [o_])
                                ST(QA[hh, :, c0:c0 + w], o_[:, :w], [o_])
                            elif grp == 4:
                                ACT(o_[:, :w], pb[:, :w], AF.Silu, [pb], [o_])
                                ST(GS[hh * 128:(hh + 1) * 128, c0:c0 + w], o_[:, :w], [o_])
                            else:
                                d_ = grp - 1
                                s_, f_, l_ = sg[cnt[0] % 3], fb[cnt[0] % 3], lfb[cnt[0] % 3]
                                col = d_ * 8 + hh
                                ACT(s_[:, :w], pb[:, :w], AF.Sigmoid, [pb], [s_])
                                TS("dve", f_[:, :w], s_[:, :w], oml[:, col:col + 1], lb[:, col:col + 1], ALU.mult, ALU.add,
                                   [s_, oml, lb], [f_])
                                ACT(l_[:, :w], f_[:, :w], AF.Ln, [f_], [l_])
                                ST(LF[d_, hh, :, c0:c0 + w], l_[:, :w], [l_])
                                TS("pool", o_[:, :w], f_[:, :w], -1.0, 1.0, ALU.mult, ALU.add, [f_], [o_])
                                ST(KF[d_, hh, :, c0:c0 + w], o_[:, :w], [o_])
                    for sub in range(w // 128):
                        v_ = vt[cnt[0] % 2]
                        cnt[0] += 1
                        for half in range(2):
                            pb = nb()
                            for kc in range(KC):
                                MM(pb[:, :], a[:, kc, sub * 128:(sub + 1) * 128],
                                   wh[kc][:, 3072 + half * 512:3072 + (half + 1) * 512], kc == 0, kc == KC - 1,
                                   [wh[kc], a], [pb])
                            CP("act" if half == 0 else "dve", v_[:, half * 512:(half + 1) * 512], pb[:, :], [pb], [v_])
                        ST(VV[c0 + sub * 128:c0 + (sub + 1) * 128, :], v_[:, :], [v_])
                p.flush()

        def phase_hg_scan(i, last):
            W = 256
            NCK = W // 32
            with ExitStack() as es:
                S = sb(es, nc, "hs_S", [128, 8, 128], F32)
                Sb = sb(es, nc, "hs_Sb", [128, 8, 128], BF16)
                onesF = sb(es, nc, "hs_1", [128, W], F32)
                msk = sb(es, nc, "hs_msk", [32, 2, 256], F32)
                q8 = [sb(es, nc, "hs_q%d" % k, [128, 8, W], BF16) for k in range(2)]
                k8 = [sb(es, nc, "hs_k%d" % k, [128, 8, W], BF16) for k in range(2)]
                lf8 = [sb(es, nc, "hs_lf%d" % k, [128, 8, W], F32) for k in range(2)]
                v32 = [sb(es, nc, "hs_v%d" % k, [32, NCK, D], BF16) for k in range(2)]
                of8 = [sb(es, nc, "hs_of%d" % k, [128, 8, W], F32) for k in range(2)]
                gs8 = [sb(es, nc, "hs_g%d" % k, [128, 8, W], BF16) for k in range(2)]
                Pp = sb(es, nc, "hs_P", [128, 8, W], F32)
                Gc = sb(es, nc, "hs_G", [128, 8, W], F32)
                eP = sb(es, nc, "hs_eP", [128, 8, W], F32)
                eM = sb(es, nc, "hs_eM", [128, 8, W], F32)
                tot = sb(es, nc, "hs_tot", [128, 8, NCK], F32)
                eT = sb(es, nc, "hs_eT", [128, 8, NCK], F32)
                qt = sb(es, nc, "hs_qt", [128, 8, W], BF16)
                kt = sb(es, nc, "hs_kt", [128, 8, W], BF16)
                ktm = [sb(es, nc, "hs_ktm%d" % k, [32, 8, 128], BF16) for k in range(2)]
                attm = [sb(es, nc, "hs_am%d" % k, [32, 256], BF16) for k in range(2)]
                oacc = sb(es, nc, "hs_oacc", [128, 8, W], F32)
                sqd = sb(es, nc, "hs_sq", [128, W], BF16)
                rsd = sb(es, nc, "hs_rsd", [128, W], F32)
                tmpo = [sb(es, nc, "hs_tmp%d" % k, [128, W], F32) for k in range(2)]
                ob8 = sb(es, nc, "hs_ob", [128, 8, W], BF16)
                MSET("dve", onesF[:, :], 1.0, [onesF])
                LD(msk[:, :, :], hgmask_in.rearrange("d s t -> s d t"), [msk])
                OFd = Buf(None, "OF_dram")
                lat = [(c, W) for c in range(CTX, NT, W)]
                nld = [0]
                for d_ in range(2):
                    tiles = [(0, CTX, True)] + [(c, w, False) for (c, w) in (lat if d_ == 0 else lat[::-1])]
                    MSET("pool", S[:, :, :], 0.0, [S])

                    def load(n, d_=d_, tiles=tiles):
                        c0, w, isc = tiles[n]
                        k_ = (nld[0] + n) % 2
                        LD(q8[k_][:, :, :w], QA[:, :, c0:c0 + w].rearrange("h p t -> p h t"), [q8[k_]])
                        LD(k8[k_][:, :, :w], KF[d_, :, :, c0:c0 + w].rearrange("h p t -> p h t"), [k8[k_]])
                        LD(lf8[k_][:, :, :w], LF[d_, :, :, c0:c0 + w].rearrange("h p t -> p h t"), [lf8[k_]])
                        LD(v32[k_][:, :, :], VV[c0:c0 + w, :].rearrange("(c p) d -> p c d", p=32), [v32[k_]])
                        if d_ == 1:
                            p.dma("sp", lambda e, o_=of8[k_][:, :, :w], i_=OF[:, c0:c0 + w].rearrange("(h p) t -> p h t", p=128):
                                  e.dma_start(out=o_, in_=i_), r=[OFd], w=[of8[k_]])
                            LD(gs8[k_][:, :, :w], GS[:, c0:c0 + w].rearrange("(h p) t -> p h t", p=128), [gs8[k_]])
                    load(0)
                    for n in range(len(tiles)):
                        if n + 1 < len(tiles):
                            load(n + 1)
                        c0, w, isc = tiles[n]
                        k_ = (nld[0] + n) % 2
                        q_, kk_, lf_, v_, of_, gs_ = q8[k_], k8[k_], lf8[k_], v32[k_], of8[k_], gs8[k_]
                        for hh in range(8):
                            p.op("dve", lambda e, hh=hh, lf_=lf_: e.tensor_tensor_scan(
                                out=Pp[:, hh, :w], data0=onesF[:, :w], data1=lf_[:, hh, :w], initial=0.0,
                                op0=ALU.mult, op1=ALU.add), r=[lf_, onesF], w=[Pp])

                        def v4(b_):
                            return b_[:, :, :w].rearrange("p h (c t) -> p h c t", t=32)
                        P4, G4, L4 = v4(Pp), v4(Gc), v4(lf_)
                        TT("dve", tot[:, :, :], P4[:, :, :, 31], P4[:, :, :, 0], ALU.subtract, [Pp], [tot])
                        TT("dve", tot[:, :, :], tot[:, :, :], L4[:, :, :, 0], ALU.add, [tot, lf_], [tot])
                        if d_ == 0:
                            TT("dve", G4, P4, P4[:, :, :, 31:32].broadcast_to([128, 8, NCK, 32]), ALU.subtract, [Pp], [Gc])
                        else:
                            TT("dve", Pp[:, :, :w], Pp[:, :, :w], lf_[:, :, :w], ALU.subtract, [Pp, lf_], [Pp])
                            TT("dve", G4, P4[:, :, :, 0:1].broadcast_to([128, 8, NCK, 32]), P4, ALU.subtract, [Pp], [Gc])
                        ACT(eP[:, :, :w], Gc[:, :, :w], AF.Exp, [Gc], [eP])
                        ACT(eM[:, :, :w], Gc[:, :, :w], AF.Exp, [Gc], [eM], scale=-1.0)
                        ACT(eT[:, :, :], tot[:, :, :], AF.Exp, [tot], [eT])
                        TT("pool", qt[:, :, :w], q_[:, :, :w], eP[:, :, :w], ALU.mult, [q_, eP], [qt])
                        TT("dve", kt[:, :, :w], kk_[:, :, :w], eM[:, :, :w], ALU.mult, [kk_, eM], [kt])
                        order = list(range(NCK) if d_ == 0 else range(NCK - 1, -1, -1))

                        def banks(c):
                            par = c % 2
                            po = PS[2 + par]
                            ptk = PS[4 + par]
                            pS = (PS[0], PS[1]) if par == 0 else (PS[6], PS[7])
                            return par, po, ptk, pS

                        def stage_a(c):
                            par, po, ptk, pS = banks(c)
                            cs_ = slice(c * 32, (c + 1) * 32)
                            ptkb = ptk[:, :].bitcast(BF16)
                            for hh in range(8):
                                TR(ptkb[0:32, hh * 128:(hh + 1) * 128], kt[:, hh, cs_], identb[:, :], [kt, identb], [ptk])
                            CP("act", ktm[par][:, :, :], ptkb[0:32, :].rearrange("p (h d) -> p h d", h=8), [ptk], [ktm[par]])
                            for hh in range(8):
                                MM(po[0:32, 256 + hh * 32:256 + (hh + 1) * 32], kt[:, hh, cs_], qt[:, hh, cs_], True, True, [kt, qt], [po])
                            TT("dve", attm[par][:, :], po[0:32, 256:512], msk[:, d_, :], ALU.mult, [po, msk], [attm[par]])
                            for hh in range(8):
                                MM(pS[hh // 4][:, (hh % 4) * 128:(hh % 4 + 1) * 128], ktm[par][:, hh, :],
                                   v_[0:32, c, hh * 128:(hh + 1) * 128], True, True, [ktm[par], v_], [pS[hh // 4]])

                        def stage_b(c):
                            par, po, ptk, pS = banks(c)
                            cs_ = slice(c * 32, (c + 1) * 32)
                            TT("dve", S[:, :, :], S[:, :, :], eT[:, :, c:c + 1].broadcast_to([128, 8, 128]), ALU.mult, [S, eT], [S])
                            CP("act", Sb[:, :, :], S[:, :, :], [S], [Sb])
                            for hh in range(8):
                                MM(po[:, hh * 32:(hh + 1) * 32], Sb[:, hh, :], qt[:, hh, cs_], True, False, [Sb, qt], [po])
                                MM(po[:, hh * 32:(hh + 1) * 32], v_[0:32, c, hh * 128:(hh + 1) * 128],
                                   attm[par][:, hh * 32:(hh + 1) * 32], False, True, [v_, attm[par]], [po])
                            po3 = po[:, 0:256].rearrange("p (h t) -> p h t", h=8)
                            if d_ == 0:
                                CP("act", oacc[:, :, cs_], po3, [po], [oacc])
                            else:
                                TT("dve", of_[:, :, cs_], po3, of_[:, :, cs_], ALU.add, [po, of_], [of_])
                            for half in range(2):
                                TT("dve", S[:, half * 4:(half + 1) * 4, :],
                                   pS[half][:, :].rearrange("p (h e) -> p h e", h=4), S[:, half * 4:(half + 1) * 4, :],
                                   ALU.add, [pS[half], S], [S])
                        stage_a(order[0])
                        for ci in range(len(order)):
                            if ci + 1 < len(order):
                                stage_a(order[ci + 1])
                            stage_b(order[ci])
                        if d_ == 0:
                            p.dma("sp", lambda e, o_=OF[:, c0:c0 + w].rearrange("(h p) t -> p h t", p=128), i_=oacc[:, :, :w]:
                                  e.dma_start(out=o_, in_=i_), r=[oacc], w=[OFd])
                        elif not (last and isc):
                            for hh in range(8):
                                sub_rstd(of_[:, hh, :w], [of_], 128, w, 1.0 / 128, ones_b[:, :], sqd, rsd)
                                t_ = tmpo[hh % 2]
                                STT(t_[:, :w], of_[:, hh, :w], vcol("hgon"), rsd[:, :w], ALU.mult, ALU.mult, [of_, vecT, rsd], [t_])
                                TT("pool", ob8[:, hh, :w], t_[:, :w], gs_[:, hh, :w], ALU.mult, [t_, gs_], [ob8])
                            ST(OT[:, c0:c0 + w].rearrange("(h p) t -> p h t", p=128), ob8[:, :, :w], [ob8])
                    nld[0] += len(tiles)
                p.flush()

        for i in range(depth):
            kind, j = i % 3, i // 3
            last = (i == depth - 1)
            if kind == 0:
                phase_mla_proj(i, j, last)
                phase_mla_attn(last)
                phase_po1(i, mla_w_o[j], last)
            elif kind == 1:
                phase_hg_proj(i)
                phase_hg_scan(i, last)
                phase_po1(i, hg_w_o[j], last)
            else:
                phase_df_proj(i, last)
                phase_df_attn(i, last)
                phase_po1(i, df_w_o[j], last)
            phase_po2(i, last)

        with ExitStack() as es:
            xin = [sb(es, nc, "pf_x%d" % i, [128, KC, 128], F32) for i in range(2)]
            xo = [sb(es, nc, "pf_o%d" % i, [128, D], F32) for i in range(2)]
            for t in range(TLAT // 128):
                xi = xin[t % 2]
                xoo = xo[t % 2]
                col = CTX + t * 128
                LD(xi[:, :, :], xT[:, col:col + 128].rearrange("(k p) t -> p k t", p=128), [xi])
                for half in range(2):
                    pt = nb()
                    for k in range(4):
                        TR(pt[:, k * 128:(k + 1) * 128], xi[:, half * 4 + k, :], ident[:, :], [xi, ident], [pt])
                    CP("dve" if half == 0 else "act", xoo[:, half * 512:(half + 1) * 512], pt[:, :], [pt], [xoo])
                ST(out[t * 128:(t + 1) * 128, :], xoo[:, :], [xoo])
            p.flush()
        print("ops", p.nops)
    return nc


def rope_table(TLAT):
    rows = TLAT // 64
    row = np.repeat(np.arange(rows, dtype=np.float32), 64)
    col = np.tile(np.arange(64, dtype=np.float32), rows)
    inv = np.power(np.float32(10000.0), -np.arange(0, 32, 2, dtype=np.float32) / np.float32(32)).astype(np.float32)
    ar = row[:, None] * inv
    ac = col[:, None] * inv
    ang = np.concatenate([ar, ar, ac, ac], axis=-1)
    cos = np.concatenate([np.ones((CTX, 64), np.float32), np.cos(ang)], axis=0).T
    sin = np.concatenate([np.zeros((CTX, 64), np.float32), np.sin(ang)], axis=0).T
    tab = np.stack([np.concatenate([cos, cos], 0), np.concatenate([sin, sin], 0)], 0)
    return np.ascontiguousarray(tab.astype(np.float32))


def rot_matrix():
    R = np.zeros((128, 128), np.float32)
    for blk in (0, 64):
        for g in (0, 32):
            for m in range(16):
                R[blk + g + m + 16, blk + g + m] = -1.0
                R[blk + g + m, blk + g + m + 16] = 1.0
    return R


def hg_mask():
    s = np.arange(32)[:, None]
    t = np.arange(32)[None, :]
    f = (s <= t).astype(np.float32)
    b = (s >= t).astype(np.float32)
    return np.ascontiguousarray(np.stack([np.tile(f, (1, 8)), np.tile(b, (1, 8))], 0))


def pack_vecs(inp, b):
    V = np.zeros((NVPAD, 128), np.float32)

    def put(name, arr):
        arr = np.asarray(arr, np.float32).reshape(-1, 128)
        V[VL[name]:VL[name] + arr.shape[0]] = arr

    def dup(v):
        return np.concatenate([v, v])

    def pad(v):
        return np.concatenate([v, np.zeros_like(v)])
    put("cc", np.stack([inp["c"][b], inp["c_ctx"]], 0))
    for i in range(4):
        put("adab%d" % i, inp["ada_b"][i])
        put("n1w%d" % i, inp["norm1_w"][i])
        put("n2w%d" % i, inp["norm2_w"][i])
    for j in range(2):
        put("mqn%d" % j, inp["mla_q_norm_w"][j])
        put("mkvn%d" % j, inp["mla_kv_norm_w"][j])
        put("mqnw%d" % j, inp["mla_qn_w"][j])
        put("mqrw%d" % j, dup(inp["mla_qr_w"][j]))
        put("mknw%d" % j, inp["mla_kn_w"][j])
        put("mkrw%d" % j, dup(inp["mla_kr_w"][j]))
    put("lb", inp["hg_lb_logits"])
    put("hgon", inp["hg_o_norm_w"][0])
    put("dqn", dup(inp["df_qn_w"][0]))
    put("dkn", dup(inp["df_kn_w"][0]))
    put("dlam", np.stack([pad(inp["df_lambda"][0][k]) for k in range(4)], 0))
    put("dsub", inp["df_sub_norm_w"][0])
    return V


def kernel(**inputs):
    inp = {k: np.asarray(v) for k, v in inputs.items()}
    B, TLAT = inp["x"].shape[0], inp["x"].shape[1]
    depth = int(inp.get("_depth", 4))
    nc = build({"TLAT": TLAT, "depth": depth})
    uq = inp["mla_w_uq"].reshape(2, 384, 8, 192)
    uq_p = np.concatenate([uq[..., :128].reshape(2, 384, 1024), uq[..., 128:].reshape(2, 384, 512)], -1)
    ukv = inp["mla_w_ukv"].reshape(2, 256, 8, 256)
    ukv_p = np.concatenate([ukv[..., :128].reshape(2, 256, 1024), ukv[..., 128:].reshape(2, 256, 1024)], -1)
    shared = {
        "ident_in": np.eye(128, dtype=np.float32), "rotm": rot_matrix(), "hgmask": hg_mask(), "rope": rope_table(TLAT),
        "ada_w": inp["ada_w"], "ffn_w1": inp["ffn_w1"], "ffn_w3": inp["ffn_w3"], "ffn_w2": inp["ffn_w2"],
        "mla_w_in": inp["mla_w_in"], "mla_w_uq": np.ascontiguousarray(uq_p), "mla_w_ukv": np.ascontiguousarray(ukv_p),
        "mla_w_o": inp["mla_w_o"], "hg_w_in": inp["hg_w_in"], "hg_w_o": inp["hg_w_o"],
        "df_w_qkv": inp["df_w_qkv"], "df_w_o": inp["df_w_o"],
    }
    in_maps = []
    for b in range(B):
        m = {"x": np.ascontiguousarray(inp["x"][b]), "ctx": np.ascontiguousarray(inp["ctx"][b]),
             "vecs": pack_vecs(inp, b)}
        m.update(shared)
        in_maps.append(m)
    res = run_bass_kernel_spmd(nc, in_maps, core_ids=list(range(B)))
    return np.stack([r["out"] for r in res.results], axis=0)
```

```python
import math
from contextlib import ExitStack
import numpy as np
import concourse.bass as bass
import concourse.mybir as mybir
from concourse.bass_utils import run_bass_kernel_spmd

F32 = mybir.dt.float32
BF16 = mybir.dt.bfloat16
AF = mybir.ActivationFunctionType
ALU = mybir.AluOpType

D = 1024
KC = 8
CTX = 256
DFF = 2816
FC = 22
EPS = 1e-6
ENG = ("pe", "act", "dve", "pool", "sp")
NDSEM = 24


class Buf:
    def __init__(self, t, name=""):
        self.t = t
        self.name = name
        self.lw = None
        self.rd = []

    def __getitem__(self, idx):
        return self.t[idx]


class Op:
    __slots__ = ("eng", "fn", "deps", "signal", "sigidx", "isdma", "dsem", "dval", "pre")

    def __init__(self, eng, fn, isdma):
        self.eng = eng
        self.fn = fn
        self.deps = []
        self.signal = False
        self.sigidx = 0
        self.isdma = isdma
        self.dsem = None
        self.dval = 0
        self.pre = None


class Prog:
    def __init__(self, nc, es):
        self.nc = nc
        self.cnt = {e: es.enter_context(nc.semaphore("cnt_" + e)) for e in ENG}
        self.cntval = {e: 0 for e in ENG}
        self.dsems = {q: [es.enter_context(nc.semaphore("d_%s_%d" % (q, i))) for i in range(NDSEM)]
                      for q in ("sp", "pool", "act")}
        self.dgen = {q: [0] * NDSEM for q in ("sp", "pool", "act")}
        self.drot = {q: 0 for q in ("sp", "pool", "act")}
        self.known = {e: {} for e in ENG}
        self.ops = []
        self.nops = 0

    def _track(self, op, r, w):
        eng = op.eng
        raw = set()
        deps = set()
        for b in r:
            if b.lw is not None:
                deps.add(b.lw)
                raw.add(b.lw)
        for b in w:
            if b.lw is not None:
                deps.add(b.lw)
            for x in b.rd:
                deps.add(x)
        for b in r:
            b.rd.append(op)
        for b in w:
            b.lw = op
            b.rd = []
        for d in deps:
            if d is op:
                continue
            if (not d.isdma) and (not op.isdma) and d.eng == eng:
                if eng == "pe":
                    continue
            op.deps.append(d)
            if not d.isdma:
                d.signal = True

    def op(self, eng, fn, r=(), w=()):
        o = Op(eng, fn, False)
        self._track(o, r, w)
        self.ops.append(o)
        return o

    def dma(self, q, fn, r=(), w=()):
        o = Op(q, fn, True)
        self._track(o, r, w)
        j = self.drot[q]
        self.drot[q] = (j + 1) % NDSEM
        g = self.dgen[q][j]
        o.pre = (j, 16 * g) if g > 0 else None
        self.dgen[q][j] = g + 1
        o.dsem = j
        o.dval = 16 * (g + 1)
        self.ops.append(o)
        return o

    def flush(self, final_wait=True):
        nc = self.nc
        ops = self.ops
        self.ops = []
        for o in ops:
            if (not o.isdma) and o.signal:
                self.cntval[o.eng] += 1
                o.sigidx = self.cntval[o.eng]
        per = {e: [o for o in ops if o.eng == e] for e in ENG}
        self.nops += len(ops)

        def emit(e, eng):
            known = self.known[e]
            for o in per[e]:
                waits = []
                if o.isdma and o.pre is not None:
                    waits.append((("d", e, o.pre[0]), self.dsems[e][o.pre[0]], o.pre[1]))
                for d in o.deps:
                    if d.isdma:
                        waits.append((("d", d.eng, d.dsem), self.dsems[d.eng][d.dsem], d.dval))
                    else:
                        waits.append((("c", d.eng), self.cnt[d.eng], d.sigidx))
                best = {}
                for k, s, v in waits:
                    if v > known.get(k, 0) and v > best.get(k, (None, 0))[1]:
                        best[k] = (s, v)
                for k, (s, v) in best.items():
                    eng.wait_ge(s, v)
                    known[k] = v
                ins = o.fn(eng)
                if o.isdma:
                    ins.then_inc(self.dsems[e][o.dsem], 16)
                elif o.signal:
                    ins.then_inc(self.cnt[e], 1)
            if e in self.dsems:
                for j in range(NDSEM):
                    v = 16 * self.dgen[e][j]
                    if v > known.get(("d", e, j), 0):
                        eng.wait_ge(self.dsems[e][j], v)
                        known[("d", e, j)] = v

        with nc.Block() as block:
            @block.tensor
            def _(eng):
                emit("pe", eng)

            @block.scalar
            def _(eng):
                emit("act", eng)

            @block.vector
            def _(eng):
                emit("dve", eng)

            @block.gpsimd
            def _(eng):
                emit("pool", eng)

            @block.sync
            def _(eng):
                emit("sp", eng)


_uid = [0]


def sb(es, nc, name, shape, dt):
    _uid[0] += 1
    name = "%s_u%d" % (name, _uid[0])
    return Buf(es.enter_context(nc.sbuf_tensor(name, list(shape), dt)), name)


def vec_layout():
    L = {}
    n = [0]

    def add(name, cnt):
        L[name] = n[0]
        n[0] += cnt
    add("cc", 16)
    for i in range(4):
        add("adab%d" % i, 48)
    for i in range(4):
        add("n1w%d" % i, 8)
    for i in range(4):
        add("n2w%d" % i, 8)
    for j in range(2):
        add("mqn%d" % j, 3)
        add("mkvn%d" % j, 2)
        add("mqnw%d" % j, 1)
        add("mqrw%d" % j, 1)
        add("mknw%d" % j, 1)
        add("mkrw%d" % j, 1)
    add("lb", 64)
    add("hgon", 1)
    add("dqn", 1)
    add("dkn", 1)
    add("dlam", 4)
    add("dsub", 1)
    return L, n[0]


VL, NVEC = vec_layout()
NVPAD = 384


def token_tiles(TLAT, width=512):
    tiles = [(0, CTX, True)] if width >= CTX else [(c, width, True) for c in range(0, CTX, width)]
    for c in range(CTX, CTX + TLAT, width):
        tiles.append((c, width, False))
    return tiles


def build(cfg):
    TLAT = cfg["TLAT"]
    NT = CTX + TLAT
    NCH = NT // 128
    depth = cfg["depth"]
    nc = bass.Bass("TRN2", target_bir_lowering=False)

    def din(name, shape, dt=F32):
        return nc.dram_tensor(name, list(shape), dt, kind="ExternalInput").ap()

    def dscr(name, shape, dt):
        return nc.dram_tensor(name, list(shape), dt, kind="Internal").ap()

    x_in = din("x", [TLAT, D])
    ctx_in = din("ctx", [CTX, D])
    vecs_in = din("vecs", [NVPAD, 128])
    ident_in = din("ident_in", [128, 128])
    rotm_in = din("rotm", [128, 128])
    rope_in = din("rope", [2, 128, NT])
    hgmask_in = din("hgmask", [2, 32, 256])
    ada_w = din("ada_w", [4, D, 6 * D])
    ffn_w1 = din("ffn_w1", [4, D, DFF])
    ffn_w3 = din("ffn_w3", [4, D, DFF])
    ffn_w2 = din("ffn_w2", [4, DFF, D])
    mla_w_in = din("mla_w_in", [2, D, 704])
    mla_w_uq = din("mla_w_uq", [2, 384, 1536])
    mla_w_ukv = din("mla_w_ukv", [2, 256, 2048])
    mla_w_o = din("mla_w_o", [2, D, D])
    hg_w_in = din("hg_w_in", [1, D, 5 * D])
    hg_w_o = din("hg_w_o", [1, D, D])
    df_w_qkv = din("df_w_qkv", [1, D, 3 * D])
    df_w_o = din("df_w_o", [1, D, D])
    out = nc.dram_tensor("out", [TLAT, D], F32, kind="ExternalOutput").ap()

    xT = dscr("xT", [D, NT], F32)
    QA = dscr("QA", [8, 128, NT], BF16)
    QB = dscr("QB", [4, 128, NT], BF16)
    KA = dscr("KA", [8, 128, NT], BF16)
    KB = dscr("KB", [64, NT], BF16)
    VV = dscr("VV", [NT, D], BF16)
    OT = dscr("OT", [D, NT], BF16)
    KF = dscr("KF", [2, 8, 128, NT], BF16)
    LF = dscr("LF", [2, 8, 128, NT], F32)
    GS = dscr("GS", [D, NT], BF16)
    OF = dscr("OF", [D, NT], F32)

    with ExitStack() as top:
        p = Prog(nc, top)
        PS = [Buf(top.enter_context(nc.psum_tensor("psb%d" % i, [128, 512], F32)), "ps%d" % i) for i in range(8)]
        psrot = [0]

        def nb():
            psrot[0] = (psrot[0] + 1) % 8
            return PS[psrot[0]]

        def MM(o, l, rh, st, sp_, r, w):
            p.op("pe", lambda e: e.matmul(o, l, rh, start=st, stop=sp_), r=r, w=w)

        def TR(o, i, idn, r, w):
            p.op("pe", lambda e: e.transpose(out=o, in_=i, identity=idn), r=r, w=w)

        def ACT(o, i, func, r, w, scale=None, bias=None):
            kw = {}
            if scale is not None:
                kw["scale"] = scale
            if bias is not None:
                kw["bias"] = bias
            p.op("act", lambda e: e.activation(out=o, in_=i, func=func, **kw), r=r, w=w)

        def TT(eng, o, a, b, op, r, w):
            p.op(eng, lambda e: e.tensor_tensor(out=o, in0=a, in1=b, op=op), r=r, w=w)

        def TS(eng, o, a, s1, s2, op0, op1, r, w):
            if op1 is None:
                p.op(eng, lambda e: e.tensor_scalar(out=o, in0=a, scalar1=s1, scalar2=None, op0=op0), r=r, w=w)
            else:
                p.op(eng, lambda e: e.tensor_scalar(out=o, in0=a, scalar1=s1, scalar2=s2, op0=op0, op1=op1), r=r, w=w)

        def STT(o, a, s, b, op0, op1, r, w):
            p.op("dve", lambda e: e.scalar_tensor_tensor(out=o, in0=a, scalar=s, in1=b, op0=op0, op1=op1), r=r, w=w)

        def CP(eng, o, i, r, w):
            if eng == "act":
                p.op("act", lambda e: e.activation(out=o, in_=i, func=AF.Copy), r=r, w=w)
            else:
                p.op(eng, lambda e: e.tensor_copy(out=o, in_=i), r=r, w=w)

        def RECIP(o, i, r, w):
            p.op("dve", lambda e: e.reciprocal(out=o, in_=i), r=r, w=w)

        def MSET(eng, o, v, w):
            p.op(eng, lambda e: e.memset(o, v), w=w)

        def LD(o, i, w, q="sp"):
            p.dma(q, lambda e: e.dma_start(out=o, in_=i), w=w)

        def ST(o, i, r, q="sp"):
            p.dma(q, lambda e: e.dma_start(out=o, in_=i), r=r)

        def LDW(o, i, w):
            p.dma("pool", lambda e: e.dma_start(out=o, in_=i, max_dma_last_dim=8192), w=w)

        ident = sb(top, nc, "ident_sb", [128, 128], F32)
        rotm = sb(top, nc, "rotm_sb", [128, 128], F32)
        ones_b = sb(top, nc, "ones_b", [128, 128], BF16)
        ones_blk = sb(top, nc, "ones_blk", [128, 128], BF16)
        ones_f = sb(top, nc, "ones_f", [128, 128], F32)
        identb = sb(top, nc, "identb", [128, 128], BF16)
        vecT = sb(top, nc, "vecT", [128, NVPAD], F32)
        sc = sb(top, nc, "silu_c", [128, 2, 8], F32)
        modT = [sb(top, nc, "modT%d" % i, [128, 2, 48], F32) for i in range(4)]
        g1 = [sb(top, nc, "g1_%d" % i, [128, 2, 8], F32) for i in range(4)]
        g2 = [sb(top, nc, "g2_%d" % i, [128, 2, 8], F32) for i in range(4)]
        LD(ident[:, :], ident_in[:, :], [ident])
        LD(rotm[:, :], rotm_in[:, :], [rotm])
        CP("dve", identb[:, :], ident[:, :], [ident], [identb])
        MSET("dve", ones_b[:, :], 1.0, [ones_b])
        MSET("dve", ones_f[:, :], 1.0, [ones_f])
        MSET("dve", ones_blk[:, :], 0.0, [ones_blk])
        MSET("dve", ones_blk[0:64, 0:64], 1.0, [ones_blk])
        MSET("dve", ones_blk[64:128, 64:128], 1.0, [ones_blk])
        with ExitStack() as es:
            vin = [sb(es, nc, "vin%d" % g, [128, 128], F32) for g in range(3)]
            for g in range(3):
                LD(vin[g][:, :], vecs_in[g * 128:(g + 1) * 128, :], [vin[g]])
                pb = nb()
                TR(pb[:, 0:128], vin[g][:, :], ident[:, :], [vin[g], ident], [pb])
                CP("dve", vecT[:, g * 128:(g + 1) * 128], pb[:, 0:128], [pb], [vecT])
            ACT(sc[:, :, :], vecT[:, VL["cc"]:VL["cc"] + 16].rearrange("p (s k) -> p s k", s=2), AF.Silu, [vecT], [sc])
            aw = [sb(es, nc, "aw%d" % i, [128, KC, 512], F32) for i in range(2)]
            n = 0
            for i in range(depth):
                pm = PS[i % 2]
                for j in range(12):
                    a_ = aw[n % 2]
                    n += 1
                    LD(a_[:, :, :], ada_w[i, :, j * 512:(j + 1) * 512].rearrange("(k p) n -> p k n", p=128), [a_])
                    for c4 in range(4):
                        ch = j * 4 + c4
                        for kc in range(KC):
                            MM(pm[:, 2 * ch:2 * ch + 2], a_[:, kc, c4 * 128:(c4 + 1) * 128], sc[:, :, kc],
                               kc == 0, kc == KC - 1, [a_, sc], [pm])
                for s in range(2):
                    TT("dve", modT[i][:, s, :], pm[:, 0:96].rearrange("p (j s) -> p j s", s=2)[:, :, s],
                       vecT[:, VL["adab%d" % i]:VL["adab%d" % i] + 48], ALU.add, [pm, vecT], [modT[i]])
                    STT(g1[i][:, s, :], modT[i][:, s, 8:16], 1.0, vecT[:, VL["n1w%d" % i]:VL["n1w%d" % i] + 8],
                        ALU.add, ALU.mult, [modT[i], vecT], [g1[i]])
                    STT(g2[i][:, s, :], modT[i][:, s, 32:40], 1.0, vecT[:, VL["n2w%d" % i]:VL["n2w%d" % i] + 8],
                        ALU.add, ALU.mult, [modT[i], vecT], [g2[i]])
            p.flush()

        def vcol(name, k=0):
            c = VL[name] + k
            return vecT[:, c:c + 1]

        with ExitStack() as es:
            xin = [sb(es, nc, "p0_x%d" % i, [128, D], F32) for i in range(2)]
            xo = [sb(es, nc, "p0_o%d" % i, [128, KC, 128], F32) for i in range(2)]
            n = 0
            for (src, nrows, base) in ((ctx_in, CTX, 0), (x_in, TLAT, CTX)):
                for t in range(nrows // 128):
                    xi = xin[n % 2]
                    xoo = xo[n % 2]
                    LD(xi[:, :], src[t * 128:(t + 1) * 128, :], [xi])
                    for half in range(2):
                        pt = nb()
                        for k in range(4):
                            kc = half * 4 + k
                            TR(pt[:, k * 128:(k + 1) * 128], xi[:, kc * 128:(kc + 1) * 128], ident[:, :], [xi, ident], [pt])
                        CP("dve" if half == 0 else "act", xoo[:, half * 4:(half + 1) * 4, :],
                           pt[:, :].rearrange("p (k t) -> p k t", k=4), [pt], [xoo])
                    col = base + t * 128
                    ST(xT[:, col:col + 128].rearrange("(k p) t -> p k t", p=128), xoo[:, :, :], [xoo])
                    n += 1
            p.flush()

        def norm_mod(xs, w, gb, s, shbase, mt, a, sqb, sqap, rstd, tmp2):
            ACT(sqap(slice(0, KC)), xs[:, :, :w], AF.Square, [xs], [sqb])
            pb = nb()
            for kc in range(KC):
                MM(pb[:, :w], ones_b[:, :], sqap(kc), kc == 0, kc == KC - 1, [sqb, ones_b], [pb])
            ACT(rstd[:, :w], pb[:, :w], AF.Sqrt, [pb], [rstd], scale=1.0 / D, bias=EPS)
            RECIP(rstd[:, :w], rstd[:, :w], [rstd], [rstd])
            for kc in range(KC):
                t_ = tmp2[kc % len(tmp2)]
                TT("dve", t_[:, :w], xs[:, kc, :w], rstd[:, :w], ALU.mult, [xs, rstd], [t_])
                ACT(a[:, kc, :w], t_[:, :w], AF.Identity, [t_, gb, mt], [a],
                    scale=gb[:, s, kc:kc + 1], bias=mt[:, s, shbase + kc:shbase + kc + 1])

        def sub_rstd(src_ap, srcbufs, P_, w, inv_n, ones_ap, sq, rstd):
            ACT(sq[:P_, :w], src_ap, AF.Square, srcbufs, [sq])
            pb = nb()
            MM(pb[:P_, :w], ones_ap, sq[:P_, :w], True, True, [sq, ones_b, ones_blk], [pb])
            ACT(rstd[:P_, :w], pb[:P_, :w], AF.Sqrt, [pb], [rstd], scale=inv_n, bias=EPS)
            RECIP(rstd[:P_, :w], rstd[:P_, :w], [rstd], [rstd])

        def phase_po1(i, w_o_dram, last):
            with ExitStack() as es:
                wo = [sb(es, nc, "wo%d" % kc, [128, D], BF16) for kc in range(KC)]
                for kc in range(KC):
                    LDW(wo[kc][:, :], w_o_dram[kc * 128:(kc + 1) * 128, :], [wo[kc]])
                xs2 = [sb(es, nc, "po1_x%d" % k, [128, KC, 512], F32) for k in range(2)]
                ot2 = [sb(es, nc, "po1_o%d" % k, [128, KC, 512], BF16) for k in range(2)]
                tiles = [t for t in token_tiles(TLAT) if not (last and t[2])]

                def load(n):
                    c0, w, isc = tiles[n]
                    LD(xs2[n % 2][:, :, :w], xT[:, c0:c0 + w].rearrange("(k p) t -> p k t", p=128), [xs2[n % 2]])
                    LD(ot2[n % 2][:, :, :w], OT[:, c0:c0 + w].rearrange("(k p) t -> p k t", p=128), [ot2[n % 2]])
                load(0)
                for n in range(len(tiles)):
                    if n + 1 < len(tiles):
                        load(n + 1)
                    c0, w, isc = tiles[n]
                    s = 1 if isc else 0
                    xs, ot = xs2[n % 2], ot2[n % 2]
                    for mc in range(KC):
                        pb = nb()
                        for kc in range(KC):
                            MM(pb[:, :w], wo[kc][:, mc * 128:(mc + 1) * 128], ot[:, kc, :w], kc == 0, kc == KC - 1,
                               [wo[kc], ot], [pb])
                        STT(xs[:, mc, :w], pb[:, :w], modT[i][:, s, 16 + mc:17 + mc], xs[:, mc, :w], ALU.mult, ALU.add,
                            [pb, xs, modT[i]], [xs])
                    ST(xT[:, c0:c0 + w].rearrange("(k p) t -> p k t", p=128), xs[:, :, :w], [xs])
                p.flush()

        def phase_po2(i, last):
            W = 256
            with ExitStack() as es:
                w1 = [sb(es, nc, "w1_%d" % kc, [128, DFF], BF16) for kc in range(KC)]
                w3 = [sb(es, nc, "w3_%d" % kc, [128, DFF], BF16) for kc in range(KC)]
                w2 = [sb(es, nc, "w2_%d" % fc, [128, D], BF16) for fc in range(FC)]
                for kc in range(KC):
                    LDW(w1[kc][:, :], ffn_w1[i, kc * 128:(kc + 1) * 128, :], [w1[kc]])
                    LDW(w3[kc][:, :], ffn_w3[i, kc * 128:(kc + 1) * 128, :], [w3[kc]])
                for fc in range(FC):
                    LDW(w2[fc][:, :], ffn_w2[i, fc * 128:(fc + 1) * 128, :], [w2[fc]])
                xs2 = [sb(es, nc, "po2_x%d" % k, [128, KC, W], F32) for k in range(2)]
                a2 = sb(es, nc, "po2_a", [128, KC, W], BF16)
                h = sb(es, nc, "po2_h", [128, FC, W], BF16)
                rstd = sb(es, nc, "po2_r", [128, W], F32)
                tmp2 = [sb(es, nc, "po2_t%d" % k, [128, W], F32) for k in range(2)]
                s1b = [sb(es, nc, "po2_s%d" % k, [128, W], BF16) for k in range(2)]
                tiles = [t for t in token_tiles(TLAT, W) if not (last and t[2])]

                def load(n):
                    c0, w, isc = tiles[n]
                    LD(xs2[n % 2][:, :, :w], xT[:, c0:c0 + w].rearrange("(k p) t -> p k t", p=128), [xs2[n % 2]])
                load(0)
                for n in range(len(tiles)):
                    if n + 1 < len(tiles):
                        load(n + 1)
                    c0, w, isc = tiles[n]
                    s = 1 if isc else 0
                    xs = xs2[n % 2]
                    norm_mod(xs, w, g2[i], s, 24, modT[i], a2, h, lambda k: h[:, k, :w], rstd, tmp2)
                    for fc in range(FC):
                        p1, p3 = nb(), nb()
                        for kc in range(KC):
                            MM(p1[:, :w], w1[kc][:, fc * 128:(fc + 1) * 128], a2[:, kc, :w], kc == 0, kc == KC - 1,
                               [w1[kc], a2], [p1])
                        for kc in range(KC):
                            MM(p3[:, :w], w3[kc][:, fc * 128:(fc + 1) * 128], a2[:, kc, :w], kc == 0, kc == KC - 1,
                               [w3[kc], a2], [p3])
                        s1 = s1b[fc % 2]
                        ACT(s1[:, :w], p1[:, :w], AF.Silu, [p1], [s1])
                        TT("dve", h[:, fc, :w], s1[:, :w], p3[:, :w], ALU.mult, [s1, p3], [h])
                    for mc in range(KC):
                        pb = nb()
                        for fc in range(FC):
                            MM(pb[:, :w], w2[fc][:, mc * 128:(mc + 1) * 128], h[:, fc, :w], fc == 0, fc == FC - 1,
                               [w2[fc], h], [pb])
                        STT(xs[:, mc, :w], pb[:, :w], modT[i][:, s, 40 + mc:41 + mc], xs[:, mc, :w], ALU.mult, ALU.add,
                            [pb, xs, modT[i]], [xs])
                    ST(xT[:, c0:c0 + w].rearrange("(k p) t -> p k t", p=128), xs[:, :, :w], [xs])
                p.flush()

        def rope_apply(t_, w, cs, ob, r1, r2):
            pr = nb()
            MM(pr[:, :w], rotm[:, :], t_[:, :w], True, True, [rotm, t_], [pr])
            TT("dve", r1[:, :w], t_[:, :w], cs[:, 0, :w], ALU.mult, [t_, cs], [r1])
            TT("dve", r2[:, :w], pr[:, :w], cs[:, 1, :w], ALU.mult, [pr, cs], [r2])
            TT("pool", ob[:, :w], r1[:, :w], r2[:, :w], ALU.add, [r1, r2], [ob])

        def phase_mla_proj(i, j, last):
            with ExitStack() as es:
                win = [sb(es, nc, "mwin%d" % kc, [128, 704], BF16) for kc in range(KC)]
                wuq = [sb(es, nc, "mwuq%d" % k, [128, 1536], BF16) for k in range(3)]
                wukv = [sb(es, nc, "mwukv%d" % k, [128, 2048], BF16) for k in range(2)]
                for kc in range(KC):
                    LDW(win[kc][:, :], mla_w_in[j, kc * 128:(kc + 1) * 128, :], [win[kc]])
                for k in range(3):
                    LDW(wuq[k][:, :], mla_w_uq[j, k * 128:(k + 1) * 128, :], [wuq[k]])
                for k in range(2):
                    LDW(wukv[k][:, :], mla_w_ukv[j, k * 128:(k + 1) * 128, :], [wukv[k]])
                xs2 = [sb(es, nc, "mp_x%d" % k, [128, KC, 512], F32) for k in range(2)]
                cs2 = [sb(es, nc, "mp_cs%d" % k, [128, 2, 512], F32) for k in range(2)]
                a = sb(es, nc, "mp_a", [128, KC, 512], BF16)
                sq8 = sb(es, nc, "mp_sq8", [128, KC, 512], BF16)
                rstd = sb(es, nc, "mp_r", [128, 512], F32)
                tmp2 = [sb(es, nc, "mp_t%d" % k, [128, 512], F32) for k in range(4)]
                cqn = sb(es, nc, "mp_cqn", [128, 3, 512], BF16)
                ckvn = sb(es, nc, "mp_ckvn", [128, 2, 512], BF16)
                sq = [sb(es, nc, "mp_sq%d" % k, [128, 512], BF16) for k in range(4)]
                rs = [sb(es, nc, "mp_rs%d" % k, [128, 512], F32) for k in range(4)]
                ob = [sb(es, nc, "mp_ob%d" % k, [128, 512], BF16) for k in range(4)]
                tf = [sb(es, nc, "mp_tf%d" % k, [128, 512], F32) for k in range(4)]
                r1 = [sb(es, nc, "mp_r1%d" % k, [128, 512], F32) for k in range(2)]
                r2 = [sb(es, nc, "mp_r2%d" % k, [128, 512], F32) for k in range(2)]
                vt = [sb(es, nc, "mp_vt%d" % k, [128, D], BF16) for k in range(2)]
                tiles = token_tiles(TLAT)
                cnt = [0]

                def load(n):
                    c0, w, isc = tiles[n]
                    LD(xs2[n % 2][:, :, :w], xT[:, c0:c0 + w].rearrange("(k p) t -> p k t", p=128), [xs2[n % 2]])
                    LD(cs2[n % 2][:, :, :w], rope_in[:, :, c0:c0 + w].rearrange("c p t -> p c t"), [cs2[n % 2]])
                load(0)
                for n in range(len(tiles)):
                    if n + 1 < len(tiles):
                        load(n + 1)
                    c0, w, isc = tiles[n]
                    s = 1 if isc else 0
                    xs, cs = xs2[n % 2], cs2[n % 2]
                    need_q = not (last and isc)
                    norm_mod(xs, w, g1[i], s, 0, modT[i], a, sq8, lambda k: sq8[:, k, :w], rstd, tmp2)
                    pc = []
                    for c in range(6):
                        if c < 3 and not need_q:
                            pc.append(None)
                            continue
                        pb = nb()
                        M = 64 if c == 5 else 128
                        for kc in range(KC):
                            MM(pb[:M, :w], win[kc][:, c * 128:c * 128 + M], a[:, kc, :w], kc == 0, kc == KC - 1,
                               [win[kc], a], [pb])
                        pc.append(pb)
                    for (lo, hi, inv, wname, dst) in ((0, 3, 1.0 / 384, "mqn%d" % j, cqn), (3, 5, 1.0 / 256, "mkvn%d" % j, ckvn)):
                        if lo == 0 and not need_q:
                            continue
                        pr = nb()
                        for c in range(lo, hi):
                            q_ = sq[cnt[0] % 4]
                            cnt[0] += 1
                            ACT(q_[:, :w], pc[c][:, :w], AF.Square, [pc[c]], [q_])
                            MM(pr[:, :w], ones_b[:, :], q_[:, :w], c == lo, c == hi - 1, [q_, ones_b], [pr])
                        r_ = rs[cnt[0] % 4]
                        ACT(r_[:, :w], pr[:, :w], AF.Sqrt, [pr], [r_], scale=inv, bias=EPS)
                        RECIP(r_[:, :w], r_[:, :w], [r_], [r_])
                        for c in range(lo, hi):
                            STT(dst[:, c - lo, :w], pc[c][:, :w], vcol(wname, c - lo), r_[:, :w], ALU.mult, ALU.mult,
                                [pc[c], vecT, r_], [dst])
                    q_, r_ = sq[cnt[0] % 4], rs[cnt[0] % 4]
                    cnt[0] += 1
                    sub_rstd(pc[5][:64, :w], [pc[5]], 64, w, 1.0 / 64, ones_b[0:64, 0:64], q_, r_)
                    t_ = tf[cnt[0] % 4]
                    MSET("pool", t_[64:128, :w], 0.0, [t_])
                    STT(t_[:64, :w], pc[5][:64, :w], vecT[0:64, VL["mkrw%d" % j]:VL["mkrw%d" % j] + 1], r_[:64, :w], ALU.mult, ALU.mult,
                        [pc[5], vecT, r_], [t_])
                    o_ = ob[cnt[0] % 4]
                    rope_apply(t_, w, cs, o_, r1[cnt[0] % 2], r2[cnt[0] % 2])
                    ST(KB[:, c0:c0 + w], o_[0:64, :w], [o_])
                    if need_q:
                        for ch in range(12):
                            pb = nb()
                            for k in range(3):
                                MM(pb[:, :w], wuq[k][:, ch * 128:(ch + 1) * 128], cqn[:, k, :w], k == 0, k == 2,
                                   [wuq[k], cqn], [pb])
                            q_, r_ = sq[cnt[0] % 4], rs[cnt[0] % 4]
                            cnt[0] += 1
                            o_ = ob[cnt[0] % 4]
                            if ch < 8:
                                sub_rstd(pb[:, :w], [pb], 128, w, 1.0 / 128, ones_b[:, :], q_, r_)
                                STT(o_[:, :w], pb[:, :w], vcol("mqnw%d" % j), r_[:, :w], ALU.mult, ALU.mult,
                                    [pb, vecT, r_], [o_])
                                ST(QA[ch, :, c0:c0 + w], o_[:, :w], [o_])
                            else:
                                sub_rstd(pb[:, :w], [pb], 128, w, 1.0 / 64, ones_blk[:, :], q_, r_)
                                t_ = tf[cnt[0] % 4]
                                STT(t_[:, :w], pb[:, :w], vcol("mqrw%d" % j), r_[:, :w], ALU.mult, ALU.mult,
                                    [pb, vecT, r_], [t_])
                                rope_apply(t_, w, cs, o_, r1[cnt[0] % 2], r2[cnt[0] % 2])
                                ST(QB[ch - 8, :, c0:c0 + w], o_[:, :w], [o_])
                    for hh in range(8):
                        pb = nb()
                        for k in range(2):
                            MM(pb[:, :w], wukv[k][:, hh * 128:(hh + 1) * 128], ckvn[:, k, :w], k == 0, k == 1,
                               [wukv[k], ckvn], [pb])
                        q_, r_ = sq[cnt[0] % 4], rs[cnt[0] % 4]
                        cnt[0] += 1
                        o_ = ob[cnt[0] % 4]
                        sub_rstd(pb[:, :w], [pb], 128, w, 1.0 / 128, ones_b[:, :], q_, r_)
                        STT(o_[:, :w], pb[:, :w], vcol("mknw%d" % j), r_[:, :w], ALU.mult, ALU.mult, [pb, vecT, r_], [o_])
                        ST(KA[hh, :, c0:c0 + w], o_[:, :w], [o_])
                    for sub in range(w // 128):
                        v_ = vt[cnt[0] % 2]
                        cnt[0] += 1
                        for half in range(2):
                            pb = nb()
                            for k in range(2):
                                MM(pb[:, :], ckvn[:, k, sub * 128:(sub + 1) * 128],
                                   wukv[k][:, 1024 + half * 512:1024 + (half + 1) * 512], k == 0, k == 1,
                                   [wukv[k], ckvn], [pb])
                            CP("act" if half == 0 else "dve", v_[:, half * 512:(half + 1) * 512], pb[:, :], [pb], [v_])
                        ST(VV[c0 + sub * 128:c0 + (sub + 1) * 128, :], v_[:, :], [v_])
                p.flush()

        def phase_mla_attn(last):
            scale = 1.0 / math.sqrt(192.0)
            with ExitStack() as es:
                kr = sb(es, nc, "ma_kr", [64, NT], BF16)
                kn2 = [sb(es, nc, "ma_kn%d" % k, [128, NT], BF16) for k in range(2)]
                vh2 = [sb(es, nc, "ma_v%d" % k, [128, NCH, 128], BF16) for k in range(2)]
                qn2 = [sb(es, nc, "ma_qn%d" % k, [128, 512], BF16) for k in range(2)]
                qr2 = [sb(es, nc, "ma_qr%d" % k, [64, 512], BF16) for k in range(2)]
                pt3 = [sb(es, nc, "ma_p%d" % k, [128, 512], BF16) for k in range(6)]
                gb = [sb(es, nc, "ma_g%d" % k, [128, 512], BF16) for k in range(6)]
                gcnt = [0]

                def sum_group(grp, w, psm, first, lastg):
                    def nxt():
                        gcnt[0] += 1
                        return gb[gcnt[0] % 6]
                    cur = list(grp)
                    while len(cur) > 1:
                        nx = []
                        for a_ in range(0, len(cur) - 1, 2):
                            g_ = nxt()
                            TT("dve", g_[:, :w], cur[a_][:, :w], cur[a_ + 1][:, :w], ALU.add, [cur[a_], cur[a_ + 1]], [g_])
                            nx.append(g_)
                        if len(cur) % 2:
                            nx.append(cur[-1])
                        cur = nx
                    MM(psm[:, :w], ones_b[:, :], cur[0][:, :w], first, lastg, [ones_b, cur[0]], [psm])
                rsum = sb(es, nc, "ma_rs", [128, 512], F32)
                ob2 = [sb(es, nc, "ma_ob%d" % k, [128, 512], BF16) for k in range(2)]
                LD(kr[:, :], KB[:, :], [kr])
                tiles = [t for t in token_tiles(TLAT) if not (last and t[2])]

                def loadh(hh):
                    LD(kn2[hh % 2][:, :], KA[hh, :, :], [kn2[hh % 2]])
                    for c_ in range(0, NCH, 16):
                        ce = min(NCH, c_ + 16)
                        LD(vh2[hh % 2][:, c_:ce, :], VV[c_ * 128:ce * 128, hh * 128:(hh + 1) * 128].rearrange("(c p) d -> p c d", p=128), [vh2[hh % 2]])
                work = [(hh, n) for hh in range(8) for n in range(len(tiles))]

                def loadq(k):
                    hh, n = work[k]
                    c0, w, isc = tiles[n]
                    LD(qn2[k % 2][:, :w], QA[hh, :, c0:c0 + w], [qn2[k % 2]])
                    LD(qr2[k % 2][:, :w], QB[hh // 2, (hh % 2) * 64:(hh % 2) * 64 + 64, c0:c0 + w], [qr2[k % 2]])
                loadh(0)
                loadq(0)
                pcnt = 0
                LA = 3
                for k in range(len(work)):
                    hh, n = work[k]
                    if n == 0 and hh + 1 < 8:
                        loadh(hh + 1)
                    if k + 1 < len(work):
                        loadq(k + 1)
                    c0, w, isc = tiles[n]
                    kn, vh, qn, qr = kn2[hh % 2], vh2[hh % 2], qn2[k % 2], qr2[k % 2]
                    nk = CTX // 128 if isc else NCH
                    po, psm = PS[(k % 2) * 2], PS[(k % 2) * 2 + 1]

                    def s_mm(c):
                        pss = PS[4 + (pcnt + c) % 4]
                        MM(pss[:, :w], kn[:, c * 128:(c + 1) * 128], qn[:, :w], True, False, [kn, qn], [pss])
                        MM(pss[:, :w], kr[:, c * 128:(c + 1) * 128], qr[:, :w], False, True, [kr, qr], [pss])
                    for c in range(min(LA, nk)):
                        s_mm(c)
                    grp = []
                    for c in range(nk):
                        if c + LA < nk:
                            s_mm(c + LA)
                        pss = PS[4 + (pcnt + c) % 4]
                        pt = pt3[(pcnt + c) % 6]
                        ACT(pt[:, :w], pss[:, :w], AF.Exp, [pss], [pt], scale=scale)
                        MM(po[:, :w], vh[:, c, :], pt[:, :w], c == 0, c == nk - 1, [vh, pt], [po])
                        grp.append(pt)
                        if len(grp) == 4 or c == nk - 1:
                            sum_group(grp, w, psm, c < 4, c == nk - 1)
                            grp = []
                    pcnt += nk
                    RECIP(rsum[:, :w], psm[:, :w], [psm], [rsum])
                    o_ = ob2[k % 2]
                    TT("dve", o_[:, :w], po[:, :w], rsum[:, :w], ALU.mult, [po, rsum], [o_])
                    ST(OT[hh * 128:(hh + 1) * 128, c0:c0 + w], o_[:, :w], [o_])
                p.flush()

        def phase_df_proj(i, last):
            with ExitStack() as es:
                wq = [sb(es, nc, "dwq%d" % kc, [128, 3 * D], BF16) for kc in range(KC)]
                for kc in range(KC):
                    LDW(wq[kc][:, :], df_w_qkv[0, kc * 128:(kc + 1) * 128, :], [wq[kc]])
                xs2 = [sb(es, nc, "dp_x%d" % k, [128, KC, 512], F32) for k in range(2)]
                cs2 = [sb(es, nc, "dp_cs%d" % k, [128, 2, 512], F32) for k in range(2)]
                a = sb(es, nc, "dp_a", [128, KC, 512], BF16)
                sq8 = sb(es, nc, "dp_sq8", [128, KC, 512], BF16)
                rstd = sb(es, nc, "dp_r", [128, 512], F32)
                tmp2 = [sb(es, nc, "dp_t%d" % k, [128, 512], F32) for k in range(4)]
                sq = [sb(es, nc, "dp_sq%d" % k, [128, 512], BF16) for k in range(4)]
                rs = [sb(es, nc, "dp_rs%d" % k, [128, 512], F32) for k in range(4)]
                ob = [sb(es, nc, "dp_ob%d" % k, [128, 512], BF16) for k in range(4)]
                tf = [sb(es, nc, "dp_tf%d" % k, [128, 512], F32) for k in range(4)]
                r1 = [sb(es, nc, "dp_r1%d" % k, [128, 512], F32) for k in range(2)]
                r2 = [sb(es, nc, "dp_r2%d" % k, [128, 512], F32) for k in range(2)]
                vt = [sb(es, nc, "dp_vt%d" % k, [128, D], BF16) for k in range(2)]
                tiles = token_tiles(TLAT)
                cnt = [0]

                def load(n):
                    c0, w, isc = tiles[n]
                    LD(xs2[n % 2][:, :, :w], xT[:, c0:c0 + w].rearrange("(k p) t -> p k t", p=128), [xs2[n % 2]])
                    LD(cs2[n % 2][:, :, :w], rope_in[:, :, c0:c0 + w].rearrange("c p t -> p c t"), [cs2[n % 2]])
                load(0)
                for n in range(len(tiles)):
                    if n + 1 < len(tiles):
                        load(n + 1)
                    c0, w, isc = tiles[n]
                    s = 1 if isc else 0
                    xs, cs = xs2[n % 2], cs2[n % 2]
                    norm_mod(xs, w, g1[i], s, 0, modT[i], a, sq8, lambda k: sq8[:, k, :w], rstd, tmp2)
                    for ch in range(16):
                        if ch < 8 and last and isc:
                            continue
                        pb = nb()
                        for kc in range(KC):
                            MM(pb[:, :w], wq[kc][:, ch * 128:(ch + 1) * 128], a[:, kc, :w], kc == 0, kc == KC - 1,
                               [wq[kc], a], [pb])
                        q_, r_ = sq[cnt[0] % 4], rs[cnt[0] % 4]
                        cnt[0] += 1
                        o_ = ob[cnt[0] % 4]
                        t_ = tf[cnt[0] % 4]
                        sub_rstd(pb[:, :w], [pb], 128, w, 1.0 / 64, ones_blk[:, :], q_, r_)
                        STT(t_[:, :w], pb[:, :w], vcol("dqn" if ch < 8 else "dkn"), r_[:, :w], ALU.mult, ALU.mult,
                            [pb, vecT, r_], [t_])
                        rope_apply(t_, w, cs, o_, r1[cnt[0] % 2], r2[cnt[0] % 2])
                        dst = QA if ch < 8 else KA
                        ST(dst[ch % 8, :, c0:c0 + w], o_[:, :w], [o_])
                    for sub in range(w // 128):
                        v_ = vt[cnt[0] % 2]
                        cnt[0] += 1
                        for half in range(2):
                            pb = nb()
                            for kc in range(KC):
                                MM(pb[:, :], a[:, kc, sub * 128:(sub + 1) * 128],
                                   wq[kc][:, 2048 + half * 512:2048 + (half + 1) * 512], kc == 0, kc == KC - 1,
                                   [wq[kc], a], [pb])
                            CP("act" if half == 0 else "dve", v_[:, half * 512:(half + 1) * 512], pb[:, :], [pb], [v_])
                        ST(VV[c0 + sub * 128:c0 + (sub + 1) * 128, :], v_[:, :], [v_])
                p.flush()

        def phase_df_attn(i, last):
            scale = 1.0 / math.sqrt(64.0)
            lam_init = 0.8 - 0.6 * math.exp(-0.3 * i)
            with ExitStack() as es:
                kd2 = [sb(es, nc, "da_k%d" % k, [128, NT], BF16) for k in range(2)]
                vh2 = [sb(es, nc, "da_v%d" % k, [128, NCH, 128], BF16) for k in range(2)]
                qd2 = [sb(es, nc, "da_q%d" % k, [128, 512], BF16) for k in range(4)]
                for k_ in range(2):
                    MSET("pool", qd2[2 * k_][64:128, :], 0.0, [qd2[2 * k_]])
                    MSET("pool", qd2[2 * k_ + 1][0:64, :], 0.0, [qd2[2 * k_ + 1]])
                pt4 = [sb(es, nc, "da_p%d" % k, [128, 512], BF16) for k in range(4)]
                rsum = [sb(es, nc, "da_rs%d" % k, [128, 512], F32) for k in range(2)]
                o01 = [sb(es, nc, "da_o%d" % k, [128, 512], F32) for k in range(2)]
                od = sb(es, nc, "da_od", [128, 512], F32)
                sqd = sb(es, nc, "da_sq", [128, 512], BF16)
                rsd = sb(es, nc, "da_rsd", [128, 512], F32)
                ob2 = [sb(es, nc, "da_ob%d" % k, [128, 512], BF16) for k in range(2)]
                lam = sb(es, nc, "da_lam", [128, 4], F32)
                wsub = sb(es, nc, "da_wsub", [128, 1], F32)
                c_ = VL["dlam"]
                TT("dve", lam[:, 0:1], vecT[:, c_:c_ + 1], vecT[:, c_ + 1:c_ + 2], ALU.mult, [vecT], [lam])
                TT("dve", lam[:, 1:2], vecT[:, c_ + 2:c_ + 3], vecT[:, c_ + 3:c_ + 4], ALU.mult, [vecT, lam], [lam])
                pl = nb()
                MM(pl[:, 0:2], ones_f[:, :], lam[:, 0:2], True, True, [ones_f, lam], [pl])
                ACT(lam[:, 2:4], pl[:, 0:2], AF.Exp, [pl], [lam])
                TT("dve", lam[:, 0:1], lam[:, 3:4], lam[:, 2:3], ALU.subtract, [lam], [lam])
                TS("dve", lam[:, 0:1], lam[:, 0:1], -lam_init, None, ALU.add, None, [lam], [lam])
                TS("dve", wsub[:, :], vcol("dsub"), 1.0 - lam_init, None, ALU.mult, None, [vecT], [wsub])
                tiles = [t for t in token_tiles(TLAT) if not (last and t[2])]

                def loadh(hh):
                    LD(kd2[hh % 2][:, :], KA[hh, :, :], [kd2[hh % 2]])
                    for c_ in range(0, NCH, 16):
                        ce = min(NCH, c_ + 16)
                        LD(vh2[hh % 2][:, c_:ce, :],
                           VV[c_ * 128:ce * 128, hh * 128:(hh + 1) * 128].rearrange("(c p) d -> p c d", p=128), [vh2[hh % 2]])
                work = [(hh, n) for hh in range(8) for n in range(len(tiles))]

                def loadq(k):
                    hh, n = work[k]
                    c0, w, isc = tiles[n]
                    LD(qd2[2 * (k % 2)][0:64, :w], QA[hh, 0:64, c0:c0 + w], [qd2[2 * (k % 2)]])
                    LD(qd2[2 * (k % 2) + 1][64:128, :w], QA[hh, 64:128, c0:c0 + w], [qd2[2 * (k % 2) + 1]])
                loadh(0)
                loadq(0)
                pcnt = 0
                for k in range(len(work)):
                    hh, n = work[k]
                    if n == 0 and hh + 1 < 8:
                        loadh(hh + 1)
                    if k + 1 < len(work):
                        loadq(k + 1)
                    c0, w, isc = tiles[n]
                    kd, vh = kd2[hh % 2], vh2[hh % 2]
                    qdm = (qd2[2 * (k % 2)], qd2[2 * (k % 2) + 1])
                    nk = CTX // 128 if isc else NCH
                    LA = 3
                    nj = 2 * nk

                    def s_mm(j_):
                        c, m = j_ // 2, j_ % 2
                        pss = PS[4 + (pcnt + j_) % 4]
                        MM(pss[:, :w], kd[:, c * 128:(c + 1) * 128], qdm[m][:, :w], True, True, [kd, qdm[m]], [pss])
                    for j_ in range(min(LA, nj)):
                        s_mm(j_)
                    for j_ in range(nj):
                        if j_ + LA < nj:
                            s_mm(j_ + LA)
                        c, m = j_ // 2, j_ % 2
                        pss = PS[4 + (pcnt + j_) % 4]
                        pt = pt4[(pcnt + j_) % 4]
                        po, psm = PS[2 * m], PS[2 * m + 1]
                        ACT(pt[:, :w], pss[:, :w], AF.Exp, [pss], [pt], scale=scale)
                        MM(po[:, :w], vh[:, c, :], pt[:, :w], c == 0, c == nk - 1, [vh, pt], [po])
                        MM(psm[:, :w], ones_b[:, :], pt[:, :w], c == 0, c == nk - 1, [ones_b, pt], [psm])
                    pcnt += nj
                    for m in range(2):
                        RECIP(rsum[m][:, :w], PS[2 * m + 1][:, :w], [PS[2 * m + 1]], [rsum[m]])
                        TT("dve", o01[m][:, :w], PS[2 * m][:, :w], rsum[m][:, :w], ALU.mult, [PS[2 * m], rsum[m]], [o01[m]])
                    STT(od[:, :w], o01[1][:, :w], lam[:, 0:1], o01[0][:, :w], ALU.mult, ALU.add, [o01[0], o01[1], lam], [od])
                    sub_rstd(od[:, :w], [od], 128, w, 1.0 / 128, ones_b[:, :], sqd, rsd)
                    o_ = ob2[k % 2]
                    STT(o_[:, :w], od[:, :w], wsub[:, 0:1], rsd[:, :w], ALU.mult, ALU.mult, [od, wsub, rsd], [o_])
                    ST(OT[hh * 128:(hh + 1) * 128, c0:c0 + w], o_[:, :w], [o_])
                p.flush()

        def phase_hg_proj(i):
            with ExitStack() as es:
                wh = [sb(es, nc, "hwin%d" % kc, [128, 5 * D], BF16) for kc in range(KC)]
                for kc in range(KC):
                    LDW(wh[kc][:, :], hg_w_in[0, kc * 128:(kc + 1) * 128, :], [wh[kc]])
                ex = sb(es, nc, "hp_ex", [128, 64], F32)
                den = sb(es, nc, "hp_den", [128, 16], F32)
                lb = sb(es, nc, "hp_lb", [128, 16], F32)
                oml = sb(es, nc, "hp_oml", [128, 16], F32)
                c_ = VL["lb"]
                ACT(ex[:, :], vecT[:, c_:c_ + 64], AF.Exp, [vecT], [ex])
                TT("dve", den[:, :], ex[:, 0:16], ex[:, 16:32], ALU.add, [ex], [den])
                TT("dve", den[:, :], den[:, :], ex[:, 32:48], ALU.add, [ex, den], [den])
                TT("dve", den[:, :], den[:, :], ex[:, 48:64], ALU.add, [ex, den], [den])
                RECIP(den[:, :], den[:, :], [den], [den])
                CP("dve", lb[:, :], ex[:, 16:32], [ex], [lb])
                for l in range(2, i + 1):
                    TT("dve", lb[:, :], lb[:, :], ex[:, l * 16:(l + 1) * 16], ALU.add, [ex, lb], [lb])
                TT("dve", lb[:, :], lb[:, :], den[:, :], ALU.mult, [lb, den], [lb])
                TS("dve", oml[:, :], lb[:, :], -1.0, 1.0, ALU.mult, ALU.add, [lb], [oml])
                xs2 = [sb(es, nc, "hp_x%d" % k, [128, KC, 512], F32) for k in range(2)]
                a = sb(es, nc, "hp_a", [128, KC, 512], BF16)
                sq8 = sb(es, nc, "hp_sq8", [128, KC, 512], BF16)
                rstd = sb(es, nc, "hp_r", [128, 512], F32)
                tmp2 = [sb(es, nc, "hp_t%d" % k, [128, 512], F32) for k in range(2)]
                ob = [sb(es, nc, "hp_ob%d" % k, [128, 512], BF16) for k in range(4)]
                sg = [sb(es, nc, "hp_sg%d" % k, [128, 512], F32) for k in range(3)]
                fall = [sb(es, nc, "hp_f%d" % k, [128, 512], F32) for k in range(16)]
                lfb = [sb(es, nc, "hp_lf%d" % k, [128, 512], F32) for k in range(3)]
                vt = [sb(es, nc, "hp_vt%d" % k, [128, D], BF16) for k in range(2)]
                tiles = token_tiles(TLAT)
                cnt = [0]

                def load(n):
                    c0, w, isc = tiles[n]
                    LD(xs2[n % 2][:, :, :w], xT[:, c0:c0 + w].rearrange("(k p) t -> p k t", p=128), [xs2[n % 2]])
                load(0)
                for n in range(len(tiles)):
                    if n + 1 < len(tiles):
                        load(n + 1)
                    c0, w, isc = tiles[n]
                    s = 1 if isc else 0
                    xs = xs2[n % 2]
                    norm_mod(xs, w, g1[i], s, 0, modT[i], a, sq8, lambda k: sq8[:, k, :w], rstd, tmp2)
                    for grp in (0, 1, 2, 4):
                        for hh in range(8):
                            ch = grp * 8 + hh
                            pb = nb()
                            for kc in range(KC):
                                MM(pb[:, :w], wh[kc][:, ch * 128:(ch + 1) * 128], a[:, kc, :w], kc == 0, kc == KC - 1,
                                   [wh[kc], a], [pb])
                            cnt[0] += 1
                            o_ = ob[cnt[0] % 4]
                            if grp == 0:
                                CP("dve", o_[:, :w], pb[:, :w], [pb], [o_])
                                ST(QA[hh, :, c0:c0 + w], o_[:, :w], [o_])
                            elif grp == 4:
                                ACT(o_[:, :w], pb[:, :w], AF.Silu, [pb], [o_])
                                ST(GS[hh * 128:(hh + 1) * 128, c0:c0 + w], o_[:, :w], [o_])
                            else:
                                d_ = grp - 1
                                col = d_ * 8 + hh
                                s_, f_ = sg[cnt[0] % 3], fall[col]
                                ACT(s_[:, :w], pb[:, :w], AF.Sigmoid, [pb], [s_])
                                TS("dve", f_[:, :w], s_[:, :w], oml[:, col:col + 1], lb[:, col:col + 1], ALU.mult, ALU.add,
                                   [s_, oml, lb], [f_])
                    for col in range(16):
                        d_, hh = col // 8, col % 8
                        cnt[0] += 1
                        o_, l_, f_ = ob[cnt[0] % 4], lfb[cnt[0] % 3], fall[col]
                        ACT(l_[:, :w], f_[:, :w], AF.Ln, [f_], [l_])
                        ST(LF[d_, hh, :, c0:c0 + w], l_[:, :w], [l_])
                        TS("pool", o_[:, :w], f_[:, :w], -1.0, 1.0, ALU.mult, ALU.add, [f_], [o_])
                        ST(KF[d_, hh, :, c0:c0 + w], o_[:, :w], [o_])
                    for sub in range(w // 128):
                        v_ = vt[cnt[0] % 2]
                        cnt[0] += 1
                        for half in range(2):
                            pb = nb()
                            for kc in range(KC):
                                MM(pb[:, :], a[:, kc, sub * 128:(sub + 1) * 128],
                                   wh[kc][:, 3072 + half * 512:3072 + (half + 1) * 512], kc == 0, kc == KC - 1,
                                   [wh[kc], a], [pb])
                            CP("act" if half == 0 else "dve", v_[:, half * 512:(half + 1) * 512], pb[:, :], [pb], [v_])
                        ST(VV[c0 + sub * 128:c0 + (sub + 1) * 128, :], v_[:, :], [v_])
                p.flush()

        def phase_hg_scan(i, last):
            W = 256
            NCK = W // 32
            with ExitStack() as es:
                S = sb(es, nc, "hs_S", [128, 8, 128], F32)
                Sb = sb(es, nc, "hs_Sb", [128, 8, 128], BF16)
                onesF = sb(es, nc, "hs_1", [128, W], F32)
                msk = sb(es, nc, "hs_msk", [32, 2, 256], F32)
                q8 = [sb(es, nc, "hs_q%d" % k, [128, 8, W], BF16) for k in range(2)]
                k8 = [sb(es, nc, "hs_k%d" % k, [128, 8, W], BF16) for k in range(2)]
                lf8 = [sb(es, nc, "hs_lf%d" % k, [128, 8, W], F32) for k in range(2)]
                v32 = [sb(es, nc, "hs_v%d" % k, [32, NCK, D], BF16) for k in range(2)]
                of8 = [sb(es, nc, "hs_of%d" % k, [128, 8, W], F32) for k in range(2)]
                gs8 = [sb(es, nc, "hs_g%d" % k, [128, 8, W], BF16) for k in range(2)]
                Pp = sb(es, nc, "hs_P", [128, 8, W], F32)
                Gc = sb(es, nc, "hs_G", [128, 8, W], F32)
                eP = sb(es, nc, "hs_eP", [128, 8, W], F32)
                eM = sb(es, nc, "hs_eM", [128, 8, W], F32)
                tot = sb(es, nc, "hs_tot", [128, 8, NCK], F32)
                eT = sb(es, nc, "hs_eT", [128, 8, NCK], F32)
                qt = sb(es, nc, "hs_qt", [128, 8, W], BF16)
                kt = sb(es, nc, "hs_kt", [128, 8, W], BF16)
                ktm = [sb(es, nc, "hs_ktm%d" % k, [32, 8, 128], BF16) for k in range(2)]
                attm = [sb(es, nc, "hs_am%d" % k, [32, 256], BF16) for k in range(2)]
                oacc = sb(es, nc, "hs_oacc", [128, 8, W], F32)
                sqd = sb(es, nc, "hs_sq", [128, W], BF16)
                rsd = sb(es, nc, "hs_rsd", [128, W], F32)
                tmpo = [sb(es, nc, "hs_tmp%d" % k, [128, W], F32) for k in range(2)]
                ob8 = sb(es, nc, "hs_ob", [128, 8, W], BF16)
                MSET("dve", onesF[:, :], 1.0, [onesF])
                LD(msk[:, :, :], hgmask_in.rearrange("d s t -> s d t"), [msk])
                OFd = Buf(None, "OF_dram")
                lat = [(c, W) for c in range(CTX, NT, W)]
                nld = [0]
                for d_ in range(2):
                    tiles = [(0, CTX, True)] + [(c, w, False) for (c, w) in (lat if d_ == 0 else lat[::-1])]
                    MSET("pool", S[:, :, :], 0.0, [S])

                    def load(n, d_=d_, tiles=tiles):
                        c0, w, isc = tiles[n]
                        k_ = (nld[0] + n) % 2
                        LD(q8[k_][:, :, :w], QA[:, :, c0:c0 + w].rearrange("h p t -> p h t"), [q8[k_]])
                        LD(k8[k_][:, :, :w], KF[d_, :, :, c0:c0 + w].rearrange("h p t -> p h t"), [k8[k_]])
                        LD(lf8[k_][:, :, :w], LF[d_, :, :, c0:c0 + w].rearrange("h p t -> p h t"), [lf8[k_]])
                        LD(v32[k_][:, :, :], VV[c0:c0 + w, :].rearrange("(c p) d -> p c d", p=32), [v32[k_]])
                        if d_ == 1:
                            p.dma("sp", lambda e, o_=of8[k_][:, :, :w], i_=OF[:, c0:c0 + w].rearrange("(h p) t -> p h t", p=128):
                                  e.dma_start(out=o_, in_=i_), r=[OFd], w=[of8[k_]])
                            LD(gs8[k_][:, :, :w], GS[:, c0:c0 + w].rearrange("(h p) t -> p h t", p=128), [gs8[k_]])
                    load(0)
                    for n in range(len(tiles)):
                        if n + 1 < len(tiles):
                            load(n + 1)
                        c0, w, isc = tiles[n]
                        k_ = (nld[0] + n) % 2
                        q_, kk_, lf_, v_, of_, gs_ = q8[k_], k8[k_], lf8[k_], v32[k_], of8[k_], gs8[k_]
                        for hh in range(8):
                            p.op("dve", lambda e, hh=hh, lf_=lf_: e.tensor_tensor_scan(
                                out=Pp[:, hh, :w], data0=onesF[:, :w], data1=lf_[:, hh, :w], initial=0.0,
                                op0=ALU.mult, op1=ALU.add), r=[lf_, onesF], w=[Pp])

                        def v4(b_):
                            return b_[:, :, :w].rearrange("p h (c t) -> p h c t", t=32)
                        P4, G4, L4 = v4(Pp), v4(Gc), v4(lf_)
                        TT("dve", tot[:, :, :], P4[:, :, :, 31], P4[:, :, :, 0], ALU.subtract, [Pp], [tot])
                        TT("dve", tot[:, :, :], tot[:, :, :], L4[:, :, :, 0], ALU.add, [tot, lf_], [tot])
                        if d_ == 0:
                            TT("dve", G4, P4, P4[:, :, :, 31:32].broadcast_to([128, 8, NCK, 32]), ALU.subtract, [Pp], [Gc])
                        else:
                            TT("dve", Pp[:, :, :w], Pp[:, :, :w], lf_[:, :, :w], ALU.subtract, [Pp, lf_], [Pp])
                            TT("dve", G4, P4[:, :, :, 0:1].broadcast_to([128, 8, NCK, 32]), P4, ALU.subtract, [Pp], [Gc])
                        ACT(eP[:, :, :w], Gc[:, :, :w], AF.Exp, [Gc], [eP])
                        ACT(eM[:, :, :w], Gc[:, :, :w], AF.Exp, [Gc], [eM], scale=-1.0)
                        ACT(eT[:, :, :], tot[:, :, :], AF.Exp, [tot], [eT])
                        TT("pool", qt[:, :, :w], q_[:, :, :w], eP[:, :, :w], ALU.mult, [q_, eP], [qt])
                        TT("dve", kt[:, :, :w], kk_[:, :, :w], eM[:, :, :w], ALU.mult, [kk_, eM], [kt])
                        order = range(NCK) if d_ == 0 else range(NCK - 1, -1, -1)
                        for c in order:
                            cs_ = slice(c * 32, (c + 1) * 32)
                            par = c % 2
                            po = PS[2 + par]
                            ptk = PS[4 + par]
                            ptkb = ptk[:, :].bitcast(BF16)
                            pS = (PS[0], PS[1]) if par == 0 else (PS[6], PS[7])
                            TT("dve", S[:, :, :], S[:, :, :], eT[:, :, c:c + 1].broadcast_to([128, 8, 128]), ALU.mult, [S, eT], [S])
                            CP("act", Sb[:, :, :], S[:, :, :], [S], [Sb])
                            for hh in range(8):
                                TR(ptkb[0:32, hh * 128:(hh + 1) * 128], kt[:, hh, cs_], identb[:, :], [kt, identb], [ptk])
                            CP("act", ktm[par][:, :, :], ptkb[0:32, :].rearrange("p (h d) -> p h d", h=8), [ptk], [ktm[par]])
                            for hh in range(8):
                                MM(po[0:32, 256 + hh * 32:256 + (hh + 1) * 32], kt[:, hh, cs_], qt[:, hh, cs_], True, True, [kt, qt], [po])
                            TT("dve", attm[par][:, :], po[0:32, 256:512], msk[:, d_, :], ALU.mult, [po, msk], [attm[par]])
                            for hh in range(8):
                                MM(po[:, hh * 32:(hh + 1) * 32], Sb[:, hh, :], qt[:, hh, cs_], True, False, [Sb, qt], [po])
                                MM(po[:, hh * 32:(hh + 1) * 32], v_[0:32, c, hh * 128:(hh + 1) * 128],
                                   attm[par][:, hh * 32:(hh + 1) * 32], False, True, [v_, attm[par]], [po])
                            po3 = po[:, 0:256].rearrange("p (h t) -> p h t", h=8)
                            if d_ == 0:
                                CP("act", oacc[:, :, cs_], po3, [po], [oacc])
                            else:
                                TT("dve", of_[:, :, cs_], po3, of_[:, :, cs_], ALU.add, [po, of_], [of_])
                            for hh in range(8):
                                MM(pS[hh // 4][:, (hh % 4) * 128:(hh % 4 + 1) * 128], ktm[par][:, hh, :],
                                   v_[0:32, c, hh * 128:(hh + 1) * 128], True, True, [ktm[par], v_], [pS[hh // 4]])
                            for half in range(2):
                                TT("dve", S[:, half * 4:(half + 1) * 4, :],
                                   pS[half][:, :].rearrange("p (h e) -> p h e", h=4), S[:, half * 4:(half + 1) * 4, :],
                                   ALU.add, [pS[half], S], [S])
                        if d_ == 0:
                            p.dma("sp", lambda e, o_=OF[:, c0:c0 + w].rearrange("(h p) t -> p h t", p=128), i_=oacc[:, :, :w]:
                                  e.dma_start(out=o_, in_=i_), r=[oacc], w=[OFd])
                        elif not (last and isc):
                            for hh in range(8):
                                sub_rstd(of_[:, hh, :w], [of_], 128, w, 1.0 / 128, ones_b[:, :], sqd, rsd)
                                t_ = tmpo[hh % 2]
                                STT(t_[:, :w], of_[:, hh, :w], vcol("hgon"), rsd[:, :w], ALU.mult, ALU.mult, [of_, vecT, rsd], [t_])
                                TT("pool", ob8[:, hh, :w], t_[:, :w], gs_[:, hh, :w], ALU.mult, [t_, gs_], [ob8])
                            ST(OT[:, c0:c0 + w].rearrange("(h p) t -> p h t", p=128), ob8[:, :, :w], [ob8])
                    nld[0] += len(tiles)
                p.flush()

        for i in range(depth):
            kind, j = i % 3, i // 3
            last = (i == depth - 1)
            if kind == 0:
                phase_mla_proj(i, j, last)
                phase_mla_attn(last)
                phase_po1(i, mla_w_o[j], last)
            elif kind == 1:
                phase_hg_proj(i)
                phase_hg_scan(i, last)
                phase_po1(i, hg_w_o[j], last)
            else:
                phase_df_proj(i, last)
                phase_df_attn(i, last)
                phase_po1(i, df_w_o[j], last)
            phase_po2(i, last)

        with ExitStack() as es:
            xin = [sb(es, nc, "pf_x%d" % i, [128, KC, 128], F32) for i in range(2)]
            xo = [sb(es, nc, "pf_o%d" % i, [128, D], F32) for i in range(2)]
            for t in range(TLAT // 128):
                xi = xin[t % 2]
                xoo = xo[t % 2]
                col = CTX + t * 128
                LD(xi[:, :, :], xT[:, col:col + 128].rearrange("(k p) t -> p k t", p=128), [xi])
                for half in range(2):
                    pt = nb()
                    for k in range(4):
                        TR(pt[:, k * 128:(k + 1) * 128], xi[:, half * 4 + k, :], ident[:, :], [xi, ident], [pt])
                    CP("dve" if half == 0 else "act", xoo[:, half * 512:(half + 1) * 512], pt[:, :], [pt], [xoo])
                ST(out[t * 128:(t + 1) * 128, :], xoo[:, :], [xoo])
            p.flush()
        print("ops", p.nops)
    return nc


def rope_table(TLAT):
    rows = TLAT // 64
    row = np.repeat(np.arange(rows, dtype=np.float32), 64)
    col = np.tile(np.arange(64, dtype=np.float32), rows)
    inv = np.power(np.float32(10000.0), -np.arange(0, 32, 2, dtype=np.float32) / np.float32(32)).astype(np.float32)
    ar = row[:, None] * inv
    ac = col[:, None] * inv
    ang = np.concatenate([ar, ar, ac, ac], axis=-1)
    cos = np.concatenate([np.ones((CTX, 64), np.float32), np.cos(ang)], axis=0).T
    sin = np.concatenate([np.zeros((CTX, 64), np.float32), np.sin(ang)], axis=0).T
    tab = np.stack([np.concatenate([cos, cos], 0), np.concatenate([sin, sin], 0)], 0)
    return np.ascontiguousarray(tab.astype(np.float32))


def rot_matrix():
    R = np.zeros((128, 128), np.float32)
    for blk in (0, 64):
        for g in (0, 32):
            for m in range(16):
                R[blk + g + m + 16, blk + g + m] = -1.0
                R[blk + g + m, blk + g + m + 16] = 1.0
    return R


def hg_mask():
    s = np.arange(32)[:, None]
    t = np.arange(32)[None, :]
    f = (s <= t).astype(np.float32)
    b = (s >= t).astype(np.float32)
    return np.ascontiguousarray(np.stack([np.tile(f, (1, 8)), np.tile(b, (1, 8))], 0))


def pack_vecs(inp, b):
    V = np.zeros((NVPAD, 128), np.float32)

    def put(name, arr):
        arr = np.asarray(arr, np.float32).reshape(-1, 128)
        V[VL[name]:VL[name] + arr.shape[0]] = arr

    def dup(v):
        return np.concatenate([v, v])

    def pad(v):
        return np.concatenate([v, np.zeros_like(v)])
    put("cc", np.stack([inp["c"][b], inp["c_ctx"]], 0))
    for i in range(4):
        put("adab%d" % i, inp["ada_b"][i])
        put("n1w%d" % i, inp["norm1_w"][i])
        put("n2w%d" % i, inp["norm2_w"][i])
    for j in range(2):
        put("mqn%d" % j, inp["mla_q_norm_w"][j])
        put("mkvn%d" % j, inp["mla_kv_norm_w"][j])
        put("mqnw%d" % j, inp["mla_qn_w"][j])
        put("mqrw%d" % j, dup(inp["mla_qr_w"][j]))
        put("mknw%d" % j, inp["mla_kn_w"][j])
        put("mkrw%d" % j, dup(inp["mla_kr_w"][j]))
    put("lb", inp["hg_lb_logits"])
    put("hgon", inp["hg_o_norm_w"][0])
    put("dqn", dup(inp["df_qn_w"][0]))
    put("dkn", dup(inp["df_kn_w"][0]))
    put("dlam", np.stack([pad(inp["df_lambda"][0][k]) for k in range(4)], 0))
    put("dsub", inp["df_sub_norm_w"][0])
    return V


def kernel(**inputs):
    inp = {k: np.asarray(v) for k, v in inputs.items()}
    B, TLAT = inp["x"].shape[0], inp["x"].shape[1]
    depth = int(inp.get("_depth", 4))
    nc = build({"TLAT": TLAT, "depth": depth})
    uq = inp["mla_w_uq"].reshape(2, 384, 8, 192)
    uq_p = np.concatenate([uq[..., :128].reshape(2, 384, 1024), uq[..., 128:].reshape(2, 384, 512)], -1)
    ukv = inp["mla_w_ukv"].reshape(2, 256, 8, 256)
    ukv_p = np.concatenate([ukv[..., :128].reshape(2, 256, 1024), ukv[..., 128:].reshape(2, 256, 1024)], -1)
    shared = {
        "ident_in": np.eye(128, dtype=np.float32), "rotm": rot_matrix(), "hgmask": hg_mask(), "rope": rope_table(TLAT),
        "ada_w": inp["ada_w"], "ffn_w1": inp["ffn_w1"], "ffn_w3": inp["ffn_w3"], "ffn_w2": inp["ffn_w2"],
        "mla_w_in": inp["mla_w_in"], "mla_w_uq": np.ascontiguousarray(uq_p), "mla_w_ukv": np.ascontiguousarray(ukv_p),
        "mla_w_o": inp["mla_w_o"], "hg_w_in": inp["hg_w_in"], "hg_w_o": inp["hg_w_o"],
        "df_w_qkv": inp["df_w_qkv"], "df_w_o": inp["df_w_o"],
    }
    in_maps = []
    for b in range(B):
        m = {"x": np.ascontiguousarray(inp["x"][b]), "ctx": np.ascontiguousarray(inp["ctx"][b]),
             "vecs": pack_vecs(inp, b)}
        m.update(shared)
        in_maps.append(m)
    res = run_bass_kernel_spmd(nc, in_maps, core_ids=list(range(B)))
    return np.stack([r["out"] for r in res.results], axis=0)
```
